# Optimizing a Trainium2 kernel written in Bass

```python
import math
import jax, jax.numpy as jnp
from jax import lax
import numpy as np

D_MODEL = 1024
BATCH = 16
SEQ = 4096
DEPTH = 2

GRID_W = 64
CTX_LEN = 256
Q_BLOCK = 128
EPS = 1e-6
ROPE_BASE = 10000.0

H_A = 8
Q_LORA = 384
KV_LORA = 256
NOPE_A = 64
ROPE_A = 32
V_A = 64

H_B = 4
DH_B = 64
V_B = 2 * DH_B

D_FF = 2816
CONV_W = 3

QKV_B = H_B * 2 * DH_B
IN_W = Q_LORA + KV_LORA + ROPE_A + 3 * QKV_B + 2 * D_MODEL

kernel_name = "hybrid_mla_diffattn_convffn_prefix_ctx"


def rms_norm(x, g):
    xf = x.astype(jnp.float32)
    y = xf * lax.rsqrt(jnp.mean(xf * xf, axis=-1, keepdims=True) + EPS)
    return (y * g.astype(jnp.float32)).astype(x.dtype)


def modulate(h, shift, scale):
    return h * (1.0 + scale) + shift


def axial_rope_tables(rows, cols, dr, dtype):
    q = dr // 4
    freqs = ROPE_BASE ** (-jnp.arange(q, dtype=jnp.float32) / q)
    ang = jnp.concatenate([rows[:, None] * freqs, cols[:, None] * freqs], axis=-1)
    return jnp.cos(ang)[:, None, :].astype(dtype), jnp.sin(ang)[:, None, :].astype(dtype)


def apply_axial_rope(x, cos, sin):
    q = x.shape[-1] // 4
    xr1, xr2, xc1, xc2 = jnp.split(x, 4, axis=-1)
    cr, cc = cos[..., :q], cos[..., q:]
    sr, sc = sin[..., :q], sin[..., q:]
    return jnp.concatenate([xr1 * cr - xr2 * sr, xr1 * sr + xr2 * cr,
                            xc1 * cc - xc2 * sc, xc1 * sc + xc2 * cc], axis=-1)


def map_query_blocks(fn, qs):
    B, S = qs[0].shape[:2]
    nb = S // Q_BLOCK
    blk = tuple(q.reshape(B, nb, Q_BLOCK, *q.shape[2:]).swapaxes(0, 1) for q in qs)
    out = lax.map(lambda qb: fn(*qb), blk)
    return out.swapaxes(0, 1).reshape(B, S, *out.shape[3:])


def mla_attend(qn, qr, kn, kr, v):
    scale = (NOPE_A + ROPE_A) ** -0.5

    def one(qnb, qrb):
        s = (jnp.einsum('bqhd,bkhd->bhqk', qnb, kn) + jnp.einsum('bqhr,bkr->bhqk', qrb, kr)).astype(jnp.float32)
        p = jax.nn.softmax(s * scale, axis=-1).astype(v.dtype)
        return jnp.einsum('bhqk,bkhe->bqhe', p, v)

    return map_query_blocks(one, (qn, qr))


def diff_attend(q1, q2, k1, k2, v, lam):
    scale = DH_B ** -0.5

    def one(q1b, q2b):
        s1 = jnp.einsum('bqhd,bkhd->bhqk', q1b, k1).astype(jnp.float32) * scale
        s2 = jnp.einsum('bqhd,bkhd->bhqk', q2b, k2).astype(jnp.float32) * scale
        p = jax.nn.softmax(s1, axis=-1) - lam * jax.nn.softmax(s2, axis=-1)
        return jnp.einsum('bhqk,bkhe->bqhe', p.astype(v.dtype), v)

    return map_query_blocks(one, (q1, q2))


def mixer_inputs(h, p, rope_a, rope_b):
    B, S, _ = h.shape
    z = h @ p['w_in']
    cuts = np.cumsum([Q_LORA, KV_LORA, ROPE_A, QKV_B, QKV_B, QKV_B, D_MODEL]).tolist()
    c_q, c_kv, k_r, dq, dk, dv, ga, gb = jnp.split(z, cuts, axis=-1)
    q = (rms_norm(c_q, p['g_q']) @ p['w_uq']).reshape(B, S, H_A, NOPE_A + ROPE_A)
    qn, qr = q[..., :NOPE_A], q[..., NOPE_A:]
    kv = (rms_norm(c_kv, p['g_kv']) @ p['w_ukv']).reshape(B, S, H_A, NOPE_A + V_A)
    kn, va = kv[..., :NOPE_A], kv[..., NOPE_A:]
    kr = k_r[:, :, None, :]
    dq = dq.reshape(B, S, H_B, 2, DH_B)
    dk = dk.reshape(B, S, H_B, 2, DH_B)
    q1, q2 = dq[..., 0, :], dq[..., 1, :]
    k1, k2 = dk[..., 0, :], dk[..., 1, :]
    vb = dv.reshape(B, S, H_B, V_B)
    if rope_a is not None:
        qr = apply_axial_rope(qr, *rope_a)
        kr = apply_axial_rope(kr, *rope_a)
        q1, q2, k1, k2 = (apply_axial_rope(t, *rope_b) for t in (q1, q2, k1, k2))
    return {'qn': qn, 'qr': qr, 'kn': kn, 'kr': kr[:, :, 0, :], 'va': va,
            'q1': q1, 'q2': q2, 'k1': k1, 'k2': k2, 'vb': vb, 'ga': ga, 'gb': gb}


def mixer_outputs(o_a, o_b, ga, gb, p, lam_init):
    B, S = o_a.shape[:2]
    y_a = o_a.reshape(B, S, H_A * V_A) @ p['w_br_a']
    o_b = rms_norm(o_b, p['g_sub']) * (1.0 - lam_init)
    y_b = o_b.reshape(B, S, H_B * V_B) @ p['w_br_b']
    merged = jax.nn.sigmoid(ga) * y_a + jax.nn.sigmoid(gb) * y_b
    return merged @ p['w_out']


def conv_ffn(h, p):
    u = h @ p['w_up']
    C = u.shape[-1]
    u = lax.conv_general_dilated(u, p['conv_w'][:, None, :], window_strides=(1,),
                                 padding=((CONV_W // 2, CONV_W // 2),),
                                 dimension_numbers=('NWC', 'WIO', 'NWC'),
                                 feature_group_count=C) + p['conv_b']
    a, b = jnp.split(u, 2, axis=-1)
    return (jax.nn.silu(a) * b) @ p['w_down']


def setup_inputs(seed: int = 0) -> dict:
    key = jax.random.key(seed)
    ks = jax.random.split(key, 32)
    L, D, F = DEPTH, D_MODEL, D_FF

    def nrm(k, shape, scale):
        return jax.random.normal(k, shape, jnp.float32) * scale

    def gain(k, shape):
        return 1.0 + 0.02 * jax.random.normal(k, shape, jnp.float32)

    return {
        'x': nrm(ks[0], (BATCH, SEQ, D), 1.0),
        'c': nrm(ks[1], (BATCH, D), 1.0),
        'ctx': nrm(ks[2], (BATCH, CTX_LEN, D), 1.0),
        'c_ctx': nrm(ks[3], (D,), 1.0),
        'w_ada': nrm(ks[4], (L, D, 6 * D), 0.5 * D ** -0.5),
        'b_ada': nrm(ks[5], (L, 6 * D), 0.01),
        'g_mix': gain(ks[6], (L, D)),
        'g_ffn': gain(ks[7], (L, D)),
        'w_in': nrm(ks[8], (L, D, IN_W), D ** -0.5),
        'g_q': gain(ks[9], (L, Q_LORA)),
        'w_uq': nrm(ks[10], (L, Q_LORA, H_A * (NOPE_A + ROPE_A)), Q_LORA ** -0.5),
        'g_kv': gain(ks[11], (L, KV_LORA)),
        'w_ukv': nrm(ks[12], (L, KV_LORA, H_A * (NOPE_A + V_A)), KV_LORA ** -0.5),
        'lam_q1': nrm(ks[13], (L, DH_B), 0.1),
        'lam_k1': nrm(ks[14], (L, DH_B), 0.1),
        'lam_q2': nrm(ks[15], (L, DH_B), 0.1),
        'lam_k2': nrm(ks[16], (L, DH_B), 0.1),
        'g_sub': gain(ks[17], (L, V_B)),
        'w_br_a': nrm(ks[18], (L, H_A * V_A, D), (H_A * V_A) ** -0.5),
        'w_br_b': nrm(ks[19], (L, H_B * V_B, D), (H_B * V_B) ** -0.5),
        'w_out': nrm(ks[20], (L, D, D), D ** -0.5),
        'w_up': nrm(ks[21], (L, D, 2 * F), D ** -0.5),
        'conv_w': nrm(ks[22], (L, CONV_W, 2 * F), CONV_W ** -0.5),
        'conv_b': nrm(ks[23], (L, 2 * F), 0.01),
        'w_down': nrm(ks[24], (L, F, D), F ** -0.5),
        'g_final': gain(ks[25], (D,)),
    }


def reference(x, c, ctx, c_ctx, w_ada, b_ada, g_mix, g_ffn, w_in, g_q, w_uq, g_kv, w_ukv,
              lam_q1, lam_k1, lam_q2, lam_k2, g_sub, w_br_a, w_br_b, w_out,
              w_up, conv_w, conv_b, w_down, g_final):
    B, S, D = x.shape
    ROWS = S // GRID_W
    rows = jnp.repeat(jnp.arange(ROWS), GRID_W).astype(jnp.float32)
    cols = jnp.tile(jnp.arange(GRID_W), ROWS).astype(jnp.float32)
    rope_a = axial_rope_tables(rows, cols, ROPE_A, x.dtype)
    rope_b = axial_rope_tables(rows, cols, DH_B, x.dtype)

    x_lat, x_ctx = x, ctx
    for l in range(DEPTH):
        last = l == DEPTH - 1
        p = {'w_in': w_in[l], 'g_q': g_q[l], 'w_uq': w_uq[l], 'g_kv': g_kv[l], 'w_ukv': w_ukv[l],
             'g_sub': g_sub[l], 'w_br_a': w_br_a[l], 'w_br_b': w_br_b[l], 'w_out': w_out[l],
             'w_up': w_up[l], 'conv_w': conv_w[l], 'conv_b': conv_b[l], 'w_down': w_down[l]}
        lam_init = 0.8 - 0.6 * math.exp(-0.3 * l)
        lam = (jnp.exp(jnp.sum(lam_q1[l].astype(jnp.float32) * lam_k1[l].astype(jnp.float32)))
               - jnp.exp(jnp.sum(lam_q2[l].astype(jnp.float32) * lam_k2[l].astype(jnp.float32)))
               + lam_init)

        mod_lat = (jax.nn.silu(c) @ w_ada[l] + b_ada[l])[:, None, :]
        mod_ctx = jax.nn.silu(c_ctx) @ w_ada[l] + b_ada[l]
        sh1, sc1, gt1, sh2, sc2, gt2 = jnp.split(mod_lat, 6, axis=-1)
        csh1, csc1, cgt1, csh2, csc2, cgt2 = jnp.split(mod_ctx, 6, axis=-1)

        h_lat = modulate(rms_norm(x_lat, g_mix[l]), sh1, sc1)
        h_ctx = modulate(rms_norm(x_ctx, g_mix[l]), csh1, csc1)
        pl = mixer_inputs(h_lat, p, rope_a, rope_b)
        pc = mixer_inputs(h_ctx, p, None, None)
        cat = lambda name: jnp.concatenate([pc[name], pl[name]], axis=1)
        o_a = mla_attend(pl['qn'], pl['qr'], cat('kn'), cat('kr'), cat('va'))
        o_b = diff_attend(pl['q1'], pl['q2'], cat('k1'), cat('k2'), cat('vb'), lam)
        o_lat = mixer_outputs(o_a, o_b, pl['ga'], pl['gb'], p, lam_init)
        if not last:
            oc_a = mla_attend(pc['qn'], pc['qr'], pc['kn'], pc['kr'], pc['va'])
            oc_b = diff_attend(pc['q1'], pc['q2'], pc['k1'], pc['k2'], pc['vb'], lam)
            o_ctx = mixer_outputs(oc_a, oc_b, pc['ga'], pc['gb'], p, lam_init)
            x_ctx = x_ctx + cgt1 * o_ctx
        x_lat = x_lat + gt1 * o_lat

        h_lat = modulate(rms_norm(x_lat, g_ffn[l]), sh2, sc2)
        x_lat = x_lat + gt2 * conv_ffn(h_lat, p)
        if not last:
            h_ctx = modulate(rms_norm(x_ctx, g_ffn[l]), csh2, csc2)
            x_ctx = x_ctx + cgt2 * conv_ffn(h_ctx, p)

    return rms_norm(x_lat, g_final)
```

```python
import math
from contextlib import ExitStack

import numpy as np
import concourse.bass as bass
import concourse.mybir as mybir
from concourse.bass_utils import run_bass_kernel_spmd

F32, BF16 = mybir.dt.float32, mybir.dt.bfloat16
AF = mybir.ActivationFunctionType
ALU = mybir.AluOpType

D = 1024
DEPTH = 2
GRID_W = 64
EPS = 1e-6
ROPE_BASE = 10000.0
H_A, Q_LORA, KV_LORA, NOPE_A, ROPE_A, V_A = 8, 384, 256, 64, 32, 64
H_B, DH_B, V_B = 4, 64, 128
D_FF = 2816
NIN = 5632
NUQ = 1536
NCH_UP = 44


class Res:
    __slots__ = ("w", "rs")

    def __init__(self):
        self.w = None
        self.rs = []


class Op:
    __slots__ = ("eng", "fn", "dma", "deps", "inc", "sem", "val", "done")


ENGS = ("pe", "act", "dve", "pool", "sp")


class Sched:
    def __init__(self, nc, esem, rings):
        self.nc = nc
        self.esem = esem
        self.rings = rings
        self.ring_cnt = {e: [0] * len(r) for e, r in rings.items()}
        self.ring_pos = {e: 0 for e in rings}
        self.ticks = {e: 0 for e in ENGS}
        self.known = {e: {} for e in ENGS}
        self.handles = {}
        for e, (i, h) in esem.items():
            self.handles[i] = h
        for e, r in rings.items():
            for i, h in r:
                self.handles[i] = h
        self.ops = {e: [] for e in ENGS}
        self.dma_since_barrier = []
        self.n_ops = 0

    def add(self, eng, fn, reads=(), writes=(), dma=False):
        o = Op()
        o.eng, o.fn, o.dma, o.inc, o.sem, o.val = eng, fn, dma, dma, None, None
        o.done = False
        deps = []
        for r in reads:
            p = r.w
            if p is not None and (p.dma or dma or p.eng != eng or eng != "pe"):
                deps.append(p)
        for w in writes:
            p = w.w
            if p is not None and (p.dma or dma or p.eng != eng):
                deps.append(p)
            for p in w.rs:
                if p.dma or dma or p.eng != eng:
                    deps.append(p)
        for r in reads:
            rs = r.rs
            if not dma:
                for i in range(len(rs)):
                    if (not rs[i].dma) and rs[i].eng == eng:
                        rs[i] = o
                        break
                else:
                    rs.append(o)
            else:
                rs.append(o)
        for w in writes:
            w.w = o
            w.rs = []
        deps = [p for p in deps if not p.done]
        for p in deps:
            p.inc = True
        o.deps = deps
        self.ops[eng].append(o)
        if dma:
            self.dma_since_barrier.append(o)
        self.n_ops += 1
        return o

    def barrier(self):
        last = {e: (self.ops[e][-1] if self.ops[e] else None) for e in ENGS}
        dmas = list(self.dma_since_barrier)
        for e in ENGS:
            o = Op()
            o.eng, o.fn, o.dma, o.inc, o.sem, o.val = e, None, False, False, None, None
            o.done = False
            o.deps = [p for e2, p in last.items() if p is not None and e2 != e and p.fn is not None] + dmas
            for p in o.deps:
                p.inc = True
            self.ops[e].append(o)
        self.dma_since_barrier = []

    def flush(self):
        for e in ENGS:
            for o in self.ops[e]:
                if o.dma:
                    ring = self.rings[e]
                    k = self.ring_pos[e]
                    self.ring_pos[e] = (k + 1) % len(ring)
                    self.ring_cnt[e][k] += 1
                    o.sem = ring[k][0]
                    o.val = 16 * self.ring_cnt[e][k]
                elif o.inc and o.fn is not None:
                    self.ticks[e] += 1
                    o.sem = self.esem[e][0]
                    o.val = self.ticks[e]
        with self.nc.Block() as block:
            def run(e):
                def body(eng):
                    known = self.known[e]
                    for o in self.ops[e]:
                        for p in o.deps:
                            if known.get(p.sem, 0) < p.val:
                                known[p.sem] = p.val
                                eng.wait_ge(self.handles[p.sem], p.val)
                        if o.dma and o.val > 16 and known.get(o.sem, 0) < o.val - 16:
                            known[o.sem] = o.val - 16
                            eng.wait_ge(self.handles[o.sem], o.val - 16)
                        if o.fn is not None:
                            ins = o.fn(eng)
                            if o.dma:
                                ins.then_inc(self.handles[o.sem], 16)
                            elif o.inc:
                                ins.then_inc(self.handles[o.sem], 1)
                return body
            block.tensor(run("pe"))
            block.scalar(run("act"))
            block.vector(run("dve"))
            block.gpsimd(run("pool"))
            block.sync(run("sp"))
        for e in ENGS:
            for o in self.ops[e]:
                o.done = True
        self.ops = {e: [] for e in ENGS}


class Buf:
    def __init__(self, handles):
        self.h = handles
        self.r = [Res() for _ in handles]
        self.i = -1

    def next(self):
        self.i = (self.i + 1) % len(self.h)
        return self.h[self.i], self.r[self.i]

    def cur(self):
        return self.h[self.i], self.r[self.i]


def _rope_tables(S, CTX):
    rows = np.repeat(np.arange(S // GRID_W), GRID_W).astype(np.float32)
    cols = np.tile(np.arange(GRID_W), S // GRID_W).astype(np.float32)

    def tab(dr):
        q = dr // 4
        freqs = (ROPE_BASE ** (-np.arange(q, dtype=np.float32) / q)).astype(np.float32)
        ang = np.concatenate([rows[:, None] * freqs, cols[:, None] * freqs], axis=-1)
        cos, sin = np.cos(ang).astype(np.float32), np.sin(ang).astype(np.float32)
        cr, cc, sr, sc = cos[:, :q], cos[:, q:], sin[:, :q], sin[:, q:]
        c = np.concatenate([cr, cr, cc, cc], axis=-1)
        s = np.concatenate([-sr, sr, -sc, sc], axis=-1)
        c = np.concatenate([np.ones((CTX, dr), np.float32), c], axis=0)
        s = np.concatenate([np.zeros((CTX, dr), np.float32), s], axis=0)
        return np.ascontiguousarray(c.T), np.ascontiguousarray(s.T)

    ca, sa = tab(ROPE_A)
    cb, sb = tab(DH_B)
    ropeA = np.zeros((2, 96, CTX + S), np.float32)
    ropeA[0, 64:96], ropeA[1, 64:96] = ca, sa
    ropeB = np.stack([np.concatenate([cb, cb], 0), np.concatenate([sb, sb], 0)], 0)
    return ropeA, ropeB


def _partner(dr):
    q = dr // 4
    return np.concatenate([np.arange(q, 2 * q), np.arange(0, q), np.arange(3 * q, 4 * q), np.arange(2 * q, 3 * q)])


def _win_index():
    Z = 4256
    idx = list(range(0, 640))
    pa = _partner(ROPE_A)
    idx += [Z] * 64 + list(range(640, 672)) + [Z] * 32
    idx += [Z] * 64 + list(640 + pa) + [Z] * 32
    idx += [Z] * 128
    pb = _partner(DH_B)
    for base in (672, 1184):
        for h in range(H_B):
            b0 = base + h * 128
            idx += list(range(b0, b0 + 128))
            idx += list(b0 + pb) + list(b0 + 64 + pb)
    idx += list(range(2208, 4256))
    idx += list(range(1696, 2208))
    assert len(idx) == NIN
    return np.array(idx)


def _wuq_index():
    Z = 768
    pa = _partner(ROPE_A)
    idx = []
    for h in range(H_A):
        b0 = h * 96
        idx += list(range(b0, b0 + 96))
        idx += [Z] * 64 + list(b0 + 64 + pa)
    return np.array(idx)


def _wup_index():
    idx = []
    for i in range(D_FF // 128):
        idx += list(range(i * 128, (i + 1) * 128)) + list(range(D_FF + i * 128, D_FF + (i + 1) * 128))
    return np.array(idx)


def _fm(v, nch):
    return np.ascontiguousarray(np.asarray(v, np.float32).reshape(nch, 128).T)


def prep_shared(inp, S, CTX):
    f = lambda a: np.asarray(a, np.float32)
    L = DEPTH
    zc = lambda w: np.concatenate([w, np.zeros(w.shape[:-1] + (1,), np.float32)], axis=-1)
    wi, uq, up = _win_index(), _wuq_index(), _wup_index()
    sh = {}
    sh["wada"] = np.ascontiguousarray(f(inp["w_ada"]))
    sh["win"] = np.ascontiguousarray(zc(f(inp["w_in"]))[:, :, wi])
    sh["wuq"] = np.ascontiguousarray(zc(f(inp["w_uq"]))[:, :, uq])
    wukv = f(inp["w_ukv"]).reshape(L, KV_LORA, H_A, 128)
    sh["wukvk"] = np.ascontiguousarray(wukv[..., :64].reshape(L, KV_LORA, 512))
    sh["wukvv"] = np.ascontiguousarray(wukv[..., 64:].reshape(L, KV_LORA, 512))
    sh["wbra"] = np.ascontiguousarray(f(inp["w_br_a"]))
    sh["wbrb"] = np.ascontiguousarray(f(inp["w_br_b"]))
    sh["wout"] = np.ascontiguousarray(f(inp["w_out"]))
    sh["wup"] = np.ascontiguousarray(f(inp["w_up"])[:, :, up])
    sh["wdown"] = np.ascontiguousarray(f(inp["w_down"]))
    bada = f(inp["b_ada"])
    sh["badafm"] = np.stack([_fm(bada[l], 48) for l in range(L)])
    sh["badabc"] = np.ascontiguousarray(np.broadcast_to(
        np.stack([bada[:, 2048:3072], bada[:, 5120:6144]], 1)[:, None], (L, 128, 2, 1024)))
    sh["gmix"] = np.stack([_fm(f(inp["g_mix"])[l], 8) for l in range(L)])
    sh["gffn"] = np.stack([_fm(f(inp["g_ffn"])[l], 8) for l in range(L)])
    sh["gq"] = np.stack([_fm(f(inp["g_q"])[l], 3) for l in range(L)])
    sh["gkv"] = np.stack([_fm(f(inp["g_kv"])[l], 2) for l in range(L)])
    lam = np.stack([f(inp["lam_q1"]), f(inp["lam_k1"]), f(inp["lam_q2"]), f(inp["lam_k2"])], 1)
    sh["lamv"] = np.ascontiguousarray(np.broadcast_to(lam[:, None], (L, 128, 4, 64)))
    sh["gsubbc"] = np.ascontiguousarray(np.broadcast_to(f(inp["g_sub"])[:, None], (L, 128, 128)))
    cw = f(inp["conv_w"])[:, :, up]
    sh["convw"] = np.ascontiguousarray(cw.reshape(L, 3, NCH_UP, 128).transpose(0, 3, 2, 1))
    sh["convb"] = np.stack([_fm(f(inp["conv_b"])[l][up], NCH_UP) for l in range(L)])
    sh["gfinbc"] = np.ascontiguousarray(np.broadcast_to(f(inp["g_final"])[None], (128, 1024)))
    ra, rb = _rope_tables(S, CTX)
    sh["ropeA"], sh["ropeB"] = ra, rb
    sh["ident"] = np.eye(128, dtype=np.float32)
    return sh


def prep_core(inp, core, nb):
    f = lambda a: np.asarray(a, np.float32)
    b0 = core * nb
    x = np.ascontiguousarray(f(inp["x"])[b0:b0 + nb])
    ctx = np.ascontiguousarray(f(inp["ctx"])[b0:b0 + nb])
    cs = [f(inp["c"])[b0 + j] for j in range(nb)] + [f(inp["c_ctx"])]
    cT = np.ascontiguousarray(np.stack([c.reshape(8, 128).T for c in cs], axis=-1))
    return {"x": x, "ctx": ctx, "cT": cT}


def build_program(S, CTX, NB, depth=DEPTH, debug_stop=None):
    P = CTX + S
    NKT = P // 128
    NJ = NB + 1
    nc = bass.Bass("TRN2", target_bir_lowering=False)
    dbg = debug_stop is not None
    skind = "ExternalOutput" if dbg else "Internal"

    def din(name, shape):
        return nc.dram_tensor(name, list(shape), F32, kind="ExternalInput").ap()

    def dscr(name, shape, dt=BF16, kind=None):
        return nc.dram_tensor(name, list(shape), dt, kind=kind or skind).ap()

    L = depth
    I = {}
    I["x"] = din("x", (NB, S, D))
    I["ctx"] = din("ctx", (NB, CTX, D))
    I["cT"] = din("cT", (128, 8, NJ))
    wshapes = {"wada": (DEPTH, D, 6 * D), "win": (DEPTH, D, NIN), "wuq": (DEPTH, Q_LORA, NUQ),
               "wukvk": (DEPTH, KV_LORA, 512), "wukvv": (DEPTH, KV_LORA, 512),
               "wbra": (DEPTH, 512, D), "wbrb": (DEPTH, 512, D), "wout": (DEPTH, D, D),
               "wup": (DEPTH, D, 2 * D_FF), "wdown": (DEPTH, D_FF, D)}
    for k, shp in wshapes.items():
        I[k] = din(k, shp)
    for k, shp in {"badafm": (DEPTH, 128, 48), "badabc": (DEPTH, 128, 2, 1024), "gmix": (DEPTH, 128, 8),
                   "gffn": (DEPTH, 128, 8), "gq": (DEPTH, 128, 3), "gkv": (DEPTH, 128, 2),
                   "lamv": (DEPTH, 128, 4, 64), "gsubbc": (DEPTH, 128, 128),
                   "convw": (DEPTH, 128, NCH_UP, 3), "convb": (DEPTH, 128, NCH_UP),
                   "gfinbc": (128, 1024), "ropeA": (2, 96, P), "ropeB": (2, 128, P),
                   "ident": (128, 128)}.items():
        I[k] = din(k, shp)
    out = nc.dram_tensor("out", [NB, S, D], F32, kind="ExternalOutput").ap()

    W = {k: dscr("b_" + k, shp, kind="Internal") for k, shp in wshapes.items()}
    WR = {k: [Res() for _ in range(DEPTH)] for k in wshapes}
    X1 = dscr("X1", (NB, P, D), F32)
    X2 = dscr("X2", (NB, P, D), F32)
    QA = dscr("QA", (NB, H_A, 96, P))
    KA = dscr("KA", (NB, H_A, 64, P))
    KR = dscr("KR", (NB, 32, P))
    VA = dscr("VA", (NB, P, H_A * 65))
    QB = dscr("QB", (NB, H_B, 128, P))
    KB = dscr("KB", (NB, H_B, 128, P))
    VB = dscr("VB", (NB, P, H_B * 129))
    GT = dscr("GT", (NB, 2, 8, 128, P))
    OO = dscr("OO", (NB, P, D))
    H2 = dscr("H2", (NB, 2, 8, 128, P + 2))
    RX1, RX2, RQKV, ROO, RH2 = Res(), Res(), Res(), Res(), Res()

    es = ExitStack()
    with es:
        nsem = {"n": 0}

        def sem(name):
            h = es.enter_context(nc.semaphore(name))
            nsem["n"] += 1
            return (nsem["n"], h)

        esem = {e: sem("e_" + e) for e in ("pe", "act", "dve", "pool")}
        rings = {"sp": [sem(f"r_sp{i}") for i in range(12)], "pool": [sem(f"r_pl{i}") for i in range(12)],
                 "act": [sem(f"r_ac{i}") for i in range(2)]}
        esem["sp"] = esem["pool"]
        SC = Sched(nc, esem, rings)

        uid = {"n": 0}

        def sb(stack, name, shape, dt=F32, n=1):
            uid["n"] += 1
            return Buf([stack.enter_context(nc.sbuf_tensor(f"{name}_{uid['n']}_{i}", list(shape), dt)) for i in range(n)])

        def ps(stack, name, shape, dt=F32, n=1):
            uid["n"] += 1
            return Buf([stack.enter_context(nc.psum_tensor(f"{name}_{uid['n']}_{i}", list(shape), dt)) for i in range(n)])

        pe = lambda fn, r=(), w=(): SC.add("pe", fn, r, w)
        act = lambda fn, r=(), w=(): SC.add("act", fn, r, w)
        dve = lambda fn, r=(), w=(): SC.add("dve", fn, r, w)
        pool = lambda fn, r=(), w=(): SC.add("pool", fn, r, w)
        ld = lambda fn, r=(), w=(): SC.add("sp", fn, r, w, dma=True)
        st = lambda fn, r=(), w=(): SC.add("pool", fn, r, w, dma=True)

        def OPF(m, *a, **k):
            return lambda e: getattr(e, m)(*a, **k)

        def dma(o, i):
            return OPF("dma_start", out=o, in_=i)

        ident_h, ident_r = sb(es, "ident", (128, 128), BF16).next()
        ones_h, ones_r = sb(es, "ones", (128, 128), BF16).next()
        onesf_h, onesf_r = sb(es, "onesf", (128, 128), F32).next()
        gfin_h, gfin_r = sb(es, "gfin", (128, 1024), F32).next()
        dve(OPF("memset", ones_h[:], 1.0), (), (ones_r,))
        dve(OPF("memset", onesf_h[:], 1.0), (), (onesf_r,))
        ld(dma(gfin_h[:], I["gfinbc"]), (), (gfin_r,))
        A1 = sb(es, "A1", (128, NJ, 8)); B1 = sb(es, "B1", (128, NJ, 8))
        A2 = sb(es, "A2", (128, NJ, 8)); B2 = sb(es, "B2", (128, NJ, 8))
        GTB = sb(es, "GTB", (128, NJ, 2, 1024))
        LAM = sb(es, "LAM", (128, 4))
        GSB = sb(es, "GSB", (128, 128))
        for b_ in (A1, B1, A2, B2, GTB, LAM, GSB):
            b_.next()
        GQ = sb(es, "GQ", (128, 3)); GKV = sb(es, "GKV", (128, 2)); GQ.next(); GKV.next()
        CW = sb(es, "CW", (128, NCH_UP, 3)); CB = sb(es, "CB", (128, NCH_UP)); CW.next(); CB.next()

        with ExitStack() as ph:
            w32 = sb(ph, "w32", (128, 6144), F32, n=2)
            w16 = sb(ph, "w16", (128, 6144), BF16, n=2)
            i32_h, i32_r = sb(ph, "i32", (128, 128), F32).next()
            ld(dma(i32_h[:], I["ident"]), (), (i32_r,))
            dve(OPF("tensor_copy", out=ident_h[:], in_=i32_h[:]), (i32_r,), (ident_r,))
            k_i = 0
            for l in range(depth if debug_stop != "W0" else 0):
                for k, shp in wshapes.items():
                    rows, ncol = shp[1], shp[2]
                    for r0 in range(0, rows, 128):
                        a_h, a_r = w32.next()
                        b_h, b_r = w16.next()
                        ld(dma(a_h[:, 0:ncol], I[k][l, r0:r0 + 128, :]), (), (a_r,))
                        hc = ncol // 2
                        for (c0, c1) in ((0, hc), (hc, ncol)):
                            k_i += 1
                            if k_i % 2 == 0:
                                dve(OPF("tensor_copy", out=b_h[:, c0:c1], in_=a_h[:, c0:c1]), (a_r,), (b_r,))
                            else:
                                act(OPF("activation", out=b_h[:, c0:c1], in_=a_h[:, c0:c1], func=AF.Copy), (a_r,), (b_r,))
                        st(dma(W[k][l, r0:r0 + 128, :], b_h[:, 0:ncol]), (b_r,), (WR[k][l],))
            SC.barrier()
            SC.flush()

        if debug_stop in ("W", "W0"):
            return nc

        def wslab(k, l, c0, c1):
            return W[k][l].rearrange("(k p) n -> p k n", p=128)[:, :, c0:c1]

        seqs = []
        for b in range(NB):
            seqs.append((b, True, NB, 0, CTX))
            seqs.append((b, False, b, CTX, S))

        def tiles_of(length, T=512):
            return [(t0, min(T, length - t0)) for t0 in range(0, length, T)]

        for l in range(depth):
            last = l == depth - 1
            lam_init = 0.8 - 0.6 * math.exp(-0.3 * l)
            Xin = None if l == 0 else (X2)

            def xsrc(b, is_ctx, t0, T):
                if l == 0:
                    src = I["ctx"][b] if is_ctx else I["x"][b]
                    return src[t0:t0 + T, :]
                p0 = t0 if is_ctx else CTX + t0
                return X2[b, p0:p0 + T, :]

            with ExitStack() as ph:
                cT = sb(ph, "cT", (128, 8, NJ)); cT_h, cT_r = cT.next()
                scf_h, scf_r = sb(ph, "scf", (128, 8, NJ)).next()
                scb_h, scb_r = sb(ph, "scb", (128, 8, NJ), BF16).next()
                rep_h, rep_r = sb(ph, "rep", (128, 8, NJ, 128), BF16).next()
                wsl = sb(ph, "mwsl", (128, 8, 512), BF16, n=3)
                modT_h, modT_r = sb(ph, "modT", (128, 48, NJ)).next()
                bfm_h, bfm_r = sb(ph, "bfm", (128, 48)).next()
                bbc_h, bbc_r = sb(ph, "bbc", (128, 2, 1024)).next()
                gm_h, gm_r = sb(ph, "gm", (128, 8)).next()
                gf_h, gf_r = sb(ph, "gf", (128, 8)).next()
                lv_h, lv_r = sb(ph, "lv", (128, 4, 64)).next()
                lt_h, lt_r = sb(ph, "lt", (128, 2, 64)).next()
                tmp_h, tmp_r = sb(ph, "mtmp", (128, 8)).next()
                pm = ps(ph, "pm", (128, 512), n=2)
                pg = ps(ph, "pg", (128, 512), n=2)
                ld(dma(cT_h[:], I["cT"]), (), (cT_r,))
                ld(dma(bfm_h[:], I["badafm"][l]), (), (bfm_r,))
                ld(dma(bbc_h[:], I["badabc"][l]), (), (bbc_r,))
                ld(dma(gm_h[:], I["gmix"][l]), (), (gm_r,))
                ld(dma(gf_h[:], I["gffn"][l]), (), (gf_r,))
                ld(dma(lv_h[:], I["lamv"][l]), (), (lv_r,))
                ld(dma(GSB.h[0][:], I["gsubbc"][l]), (), (GSB.r[0],))
                ld(dma(GQ.h[0][:], I["gq"][l]), (), (GQ.r[0],))
                ld(dma(GKV.h[0][:], I["gkv"][l]), (), (GKV.r[0],))
                ld(dma(CW.h[0][:], I["convw"][l]), (), (CW.r[0],))
                ld(dma(CB.h[0][:], I["convb"][l]), (), (CB.r[0],))
                act(OPF("activation", out=scf_h[:], in_=cT_h[:], func=AF.Silu), (cT_r,), (scf_r,))
                dve(OPF("tensor_copy", out=scb_h[:], in_=scf_h[:]), (scf_r,), (scb_r,))
                for kc in range(8):
                    for j in range(NJ):
                        dve(OPF("tensor_scalar", out=rep_h[:, kc, j, :], in0=onesf_h[:], scalar1=scf_h[:, kc, j:j + 1], scalar2=None,
                            op0=ALU.mult), (scf_r, onesf_r), (rep_r,))
                dve(OPF("tensor_scalar", out=GSB.h[0][:], in0=GSB.h[0][:], scalar1=float(1.0 - lam_init),
                                              scalar2=None, op0=ALU.mult), (GSB.r[0],), (GSB.r[0],))
                LAMh, LAMr = LAM.h[0], LAM.r[0]
                for m in range(2):
                    dve(OPF("tensor_tensor", out=lt_h[:, m, :], in0=lv_h[:, 2 * m, :],
                                                       in1=lv_h[:, 2 * m + 1, :], op=ALU.mult), (lv_r,), (lt_r,))
                    dve(OPF("reduce_sum", out=LAMh[:, 2 + m:3 + m], in_=lt_h[:, m, :],
                                                    axis=mybir.AxisListType.X), (lt_r,), (LAMr,))
                act(OPF("activation", out=LAMh[:, 2:4], in_=LAMh[:, 2:4], func=AF.Exp), (LAMr,), (LAMr,))
                dve(OPF("tensor_tensor", out=LAMh[:, 0:1], in0=LAMh[:, 2:3], in1=LAMh[:, 3:4],
                                              op=ALU.subtract), (LAMr,), (LAMr,))
                dve(OPF("tensor_scalar", out=LAMh[:, 0:1], in0=LAMh[:, 0:1], scalar1=float(lam_init),
                                              scalar2=None, op0=ALU.add), (LAMr,), (LAMr,))
                for sl in range(12):
                    w_h, w_r = wsl.next()
                    ld(dma(w_h[:], wslab("wada", l, sl * 512, sl * 512 + 512)), (WR["wada"][l],), (w_r,))
                    for cc in range(4):
                        ch = sl * 4 + cc
                        p_h, p_r = pm.next()
                        for kc in range(8):
                            pe(OPF("matmul", p_h[:, 0:NJ], lhsT=w_h[:, kc, cc * 128:(cc + 1) * 128], rhs=scb_h[:, kc, :],
                                start=(kc == 0), stop=(kc == 7)), (w_r, scb_r), (p_r,))
                        dve(OPF("tensor_scalar", out=modT_h[:, ch, :], in0=p_h[:, 0:NJ], scalar1=bfm_h[:, ch:ch + 1], scalar2=None,
                            op0=ALU.add), (p_r, bfm_r), (modT_r,))
                    if sl in (4, 5, 10, 11):
                        which, half = (0 if sl < 6 else 1), sl % 2
                        for j in range(NJ):
                            g_h, g_r = pg.next()
                            for kc in range(8):
                                pe(OPF("matmul", g_h[:], lhsT=rep_h[:, kc, j, :], rhs=w_h[:, kc, :],
                                    start=(kc == 0), stop=(kc == 7)), (w_r, rep_r), (g_r,))
                            dve(OPF("tensor_tensor", out=GTB.h[0][:, j, which, half * 512:(half + 1) * 512], in0=g_h[:],
                                in1=bbc_h[:, which, half * 512:(half + 1) * 512], op=ALU.add),
                                (g_r, bbc_r), (GTB.r[0],))
                for j in range(NJ):
                    for (Ab, Bb, g_h, g_r, sc0, sh0) in ((A1, B1, gm_h, gm_r, 8, 0), (A2, B2, gf_h, gf_r, 32, 24)):
                        dve(OPF("tensor_scalar", out=tmp_h[:], in0=modT_h[:, sc0:sc0 + 8, j], scalar1=1.0, scalar2=None, op0=ALU.add),
                            (modT_r,), (tmp_r,))
                        dve(OPF("tensor_tensor", out=Ab.h[0][:, j, :], in0=tmp_h[:], in1=g_h[:], op=ALU.mult),
                            (tmp_r, g_r), (Ab.r[0],))
                        dve(OPF("tensor_copy", out=Bb.h[0][:, j, :], in_=modT_h[:, sh0:sh0 + 8, j]), (modT_r,), (Bb.r[0],))
                SC.barrier()
                SC.flush()

            def norm_to_fm(x_h, x_r, ns, T, j, Ab, Bb, bufs, hT_h, hT_r, col0=0):
                ss, junk, rstd, xs, pT = bufs
                ss_h, ss_r = ss.next()
                jk_h, jk_r = junk.next()
                rs_h, rs_r = rstd.next()
                xs_h, xs_r = xs.next()
                dve(OPF("memset", ss_h[:], 0.0), (), (ss_r,))
                for s in range(ns):
                    act(OPF("activation", out=jk_h[:], in_=x_h[:, s, :], func=AF.Square,
                                                    accum_out=ss_h[:, s:s + 1]), (x_r, ss_r), (jk_r, ss_r))
                act(OPF("activation", out=rs_h[:, 0:ns], in_=ss_h[:, 0:ns], func=AF.Sqrt, scale=1.0 / D, bias=EPS),
                    (ss_r,), (rs_r,))
                dve(OPF("reciprocal", out=rs_h[:, 0:ns], in_=rs_h[:, 0:ns]), (rs_r,), (rs_r,))
                for s in range(ns):
                    dve(OPF("tensor_scalar", out=xs_h[:, s, :], in0=x_h[:, s, :], scalar1=rs_h[:, s:s + 1],
                                                       scalar2=None, op0=ALU.mult), (x_r, rs_r), (xs_r,))
                for c2 in range(4):
                    p_h, p_r = pT.next()
                    for cc in range(2):
                        c = c2 * 2 + cc
                        for s in range(ns):
                            pe(OPF("transpose", out=p_h[:, cc * 512 + s * 128: cc * 512 + (s + 1) * 128],
                                in_=xs_h[:, s, c * 128:(c + 1) * 128], identity=ident_h[:]),
                                (xs_r, ident_r), (p_r,))
                    for cc in range(2):
                        c = c2 * 2 + cc
                        act(OPF("activation", out=hT_h[:, c, col0:col0 + T], in_=p_h[:, cc * 512:cc * 512 + T], func=AF.Identity,
                            scale=Ab.h[0][:, j, c:c + 1], bias=Bb.h[0][:, j, c:c + 1]),
                            (p_r, Ab.r[0], Bb.r[0]), (hT_r,))
                return rs_h, rs_r

            with ExitStack() as ph:
                xt = sb(ph, "xt", (128, 4, 1024), F32, n=1)
                nb_ss = sb(ph, "ss", (128, 4), F32, n=2)
                nb_junk = sb(ph, "junk", (128, 1024), BF16, n=1)
                nb_rstd = sb(ph, "rstd", (128, 4), F32, n=2)
                nb_xs = sb(ph, "xs", (128, 4, 1024), BF16, n=1)
                pT = ps(ph, "pT", (128, 1024), BF16, n=2)
                hT = sb(ph, "hT", (128, 8, 512), BF16, n=1)
                wsl = sb(ph, "wsl", (128, 8, 512), BF16, n=2)
                pz = ps(ph, "pz", (128, 512), F32, n=4)
                pn = ps(ph, "pn", (128, 512), F32, n=1)
                ptm = ps(ph, "ptm", (128, 512), F32, n=1)
                cq = sb(ph, "cq", (128, 5, 512), F32, n=1)
                sq = sb(ph, "sq", (128, 5, 512), BF16, n=1)
                cqn = sb(ph, "cqn", (128, 5, 512), BF16, n=1)
                rq = sb(ph, "rq", (128, 2, 512), F32, n=1)
                rA = sb(ph, "rA", (96, 2, 512), F32, n=1)
                rB = sb(ph, "rB", (128, 2, 512), F32, n=1)
                t1 = sb(ph, "t1", (128, 512), F32, n=2)
                t2 = sb(ph, "t2", (128, 512), F32, n=2)
                krs = sb(ph, "krs", (96, 512), BF16, n=2)
                qas = sb(ph, "qas", (96, 8, 512), BF16, n=1)
                kas = sb(ph, "kas", (64, 8, 512), BF16, n=1)
                qbs = sb(ph, "qbs", (128, 4, 512), BF16, n=2)
                kbs = sb(ph, "kbs", (128, 4, 512), BF16, n=2)
                gts = sb(ph, "gts", (128, 8, 512), BF16, n=2)
                vas = sb(ph, "vas", (128, 4, 8, 65), BF16, n=2)
                vbs = sb(ph, "vbs", (128, 4, 4, 129), BF16, n=2)
                wuq_h, wuq_r = sb(ph, "wuq", (128, 3, NUQ), BF16).next()
                wkk_h, wkk_r = sb(ph, "wkk", (128, 2, 512), BF16).next()
                wkv_h, wkv_r = sb(ph, "wkv", (128, 2, 512), BF16).next()
                ld(dma(wuq_h[:], wslab("wuq", l, 0, NUQ)), (WR["wuq"][l],), (wuq_r,))
                ld(dma(wkk_h[:], wslab("wukvk", l, 0, 512)), (WR["wukvk"][l],), (wkk_r,))
                ld(dma(wkv_h[:], wslab("wukvv", l, 0, 512)), (WR["wukvv"][l],), (wkv_r,))
                for i in range(2):
                    dve(OPF("memset", vas.h[i][:], 1.0), (), (vas.r[i],))
                    dve(OPF("memset", vbs.h[i][:], 1.0), (), (vbs.r[i],))
                nbufs = (nb_ss, nb_junk, nb_rstd, nb_xs, pT)
                GQh, GQr, GKVh, GKVr = GQ.h[0], GQ.r[0], GKV.h[0], GKV.r[0]

                for (b, is_ctx, j, pos0, length) in seqs:
                    for (t0, T) in tiles_of(length):
                        ns = T // 128
                        p0 = pos0 + t0
                        x_h, x_r = xt.next()
                        rdeps = () if l == 0 else (RX2,)
                        ld(dma(x_h[:, 0:ns, :], xsrc(b, is_ctx, t0, T).rearrange("(s p) d -> p s d", p=128)),
                           rdeps, (x_r,))
                        rA_h, rA_r = rA.next()
                        rB_h, rB_r = rB.next()
                        ld(dma(rA_h[64:96, :, 0:T], I["ropeA"][:, 64:96, p0:p0 + T].rearrange("c p t -> p c t")),
                           (), (rA_r,))
                        ld(dma(rB_h[:, :, 0:T], I["ropeB"][:, :, p0:p0 + T].rearrange("c p t -> p c t")),
                           (), (rB_r,))
                        hT_h, hT_r = hT.next()
                        norm_to_fm(x_h, x_r, ns, T, j, A1, B1, nbufs, hT_h, hT_r)
                        cq_h, cq_r = cq.next(); sq_h, sq_r = sq.next(); cqn_h, cqn_r = cqn.next()
                        rq_h, rq_r = rq.next()
                        krs_h, krs_r = krs.next(); qas_h, qas_r = qas.next(); kas_h, kas_r = kas.next()
                        qbs_h, qbs_r = qbs.next(); kbs_h, kbs_r = kbs.next()
                        vas_h, vas_r = vas.next(); vbs_h, vbs_r = vbs.next()

                        def mm8(p_h, p_r, w_h, w_r, off, M):
                            for kc in range(8):
                                pe(OPF("matmul", p_h[0:M, 0:T], lhsT=w_h[:, kc, off:off + M],
                                                             rhs=hT_h[:, kc, 0:T], start=(kc == 0), stop=(kc == 7)),
                                   (w_r, hT_r), (p_r,))

                        def rope_evac(pm_h, pm_r, pp_h, pp_r, r_h, r_r, lo, hi, out_ap, out_r):
                            a_h, a_r = t1.next()
                            b_h, b_r = t2.next()
                            dve(OPF("tensor_tensor", out=a_h[lo:hi, 0:T], in0=pm_h[lo:hi, 0:T],
                                                          in1=r_h[lo:hi, 0, 0:T], op=ALU.mult), (pm_r, r_r), (a_r,))
                            dve(OPF("tensor_tensor", out=b_h[lo:hi, 0:T], in0=pp_h[lo:hi, 0:T],
                                                          in1=r_h[lo:hi, 1, 0:T], op=ALU.mult), (pp_r, r_r), (b_r,))
                            dve(OPF("tensor_tensor", out=out_ap, in0=a_h[lo:hi, 0:T], in1=b_h[lo:hi, 0:T],
                                                          op=ALU.add), (a_r, b_r), (out_r,))

                        for sl in range(10):
                            w_h, w_r = wsl.next()
                            ld(dma(w_h[:], wslab("win", l, sl * 512, sl * 512 + 512)), (WR["win"][l],), (w_r,))
                            if sl <= 1:
                                for cc in range(4 if sl == 0 else 1):
                                    ci = sl * 4 + cc
                                    p_h, p_r = pz.next()
                                    mm8(p_h, p_r, w_h, w_r, cc * 128, 128)
                                    act(OPF("activation", out=cq_h[:, ci, 0:T], in_=p_h[:, 0:T], func=AF.Copy), (p_r,), (cq_r,))
                                    act(OPF("activation", out=sq_h[:, ci, 0:T], in_=p_h[:, 0:T], func=AF.Square), (p_r,), (sq_r,))
                                if sl == 1:
                                    pm_h, pm_r = pz.next()
                                    mm8(pm_h, pm_r, w_h, w_r, 128, 96)
                                    pp_h, pp_r = pz.next()
                                    mm8(pp_h, pp_r, w_h, w_r, 256, 96)
                                    rope_evac(pm_h, pm_r, pp_h, pp_r, rA_h, rA_r, 64, 96, krs_h[64:96, 0:T], krs_r)
                                    st(dma(KR[b, :, p0:p0 + T], krs_h[64:96, 0:T]), (krs_r,), (RQKV,))
                            elif sl <= 5:
                                kind = (sl - 2) // 2
                                for hh in range(2):
                                    h = ((sl - 2) % 2) * 2 + hh
                                    pm_h, pm_r = pz.next()
                                    mm8(pm_h, pm_r, w_h, w_r, hh * 256, 128)
                                    pp_h, pp_r = pz.next()
                                    mm8(pp_h, pp_r, w_h, w_r, hh * 256 + 128, 128)
                                    dst_h, dst_r = (qbs_h, qbs_r) if kind == 0 else (kbs_h, kbs_r)
                                    rope_evac(pm_h, pm_r, pp_h, pp_r, rB_h, rB_r, 0, 128, dst_h[:, h, 0:T], dst_r)
                            else:
                                for cc in range(4):
                                    g = (sl - 6) * 4 + cc
                                    gk, gc = g // 8, g % 8
                                    if gc == 0:
                                        gts_h, gts_r = gts.next()
                                    p_h, p_r = pz.next()
                                    mm8(p_h, p_r, w_h, w_r, cc * 128, 128)
                                    act(OPF("activation", out=gts_h[:, gc, 0:T], in_=p_h[:, 0:T], func=AF.Sigmoid),
                                        (p_r,), (gts_r,))
                                    if gc == 7:
                                        st(dma(GT[b, gk, :, :, p0:p0 + T].rearrange("c p t -> p c t"),
                                               gts_h[:, :, 0:T]), (gts_r,), (RQKV,))
                        st(dma(QB[b, :, :, p0:p0 + T].rearrange("h p t -> p h t"), qbs_h[:, :, 0:T]), (qbs_r,), (RQKV,))
                        st(dma(KB[b, :, :, p0:p0 + T].rearrange("h p t -> p h t"), kbs_h[:, :, 0:T]), (kbs_r,), (RQKV,))
                        w_h, w_r = wsl.next()
                        ld(dma(w_h[:], wslab("win", l, 5120, 5632)), (WR["win"][l],), (w_r,))
                        for s in range(ns):
                            p_h, p_r = ptm.next()
                            for kc in range(8):
                                pe(OPF("matmul", p_h[:, :], lhsT=hT_h[:, kc, s * 128:(s + 1) * 128], rhs=w_h[:, kc, :],
                                    start=(kc == 0), stop=(kc == 7)), (w_r, hT_r), (p_r,))
                            dve(OPF("tensor_copy", out=vbs_h[:, s, :, 0:128], in_=p_h[:, :].rearrange("p (h d) -> p h d", h=4)),
                                (p_r,), (vbs_r,))
                        st(dma(VB[b, p0:p0 + T, :].rearrange("(s p) f -> p s f", p=128),
                               vbs_h[:, 0:ns].rearrange("p s h d -> p s (h d)")), (vbs_r,), (RQKV,))
                        for (k0, k1, rr, nfeat) in ((0, 3, 0, Q_LORA), (3, 5, 1, KV_LORA)):
                            p_h, p_r = pn.next()
                            for kc in range(k0, k1):
                                pe(OPF("matmul", p_h[:, 0:T], lhsT=ones_h[:], rhs=sq_h[:, kc, 0:T], start=(kc == k0),
                                    stop=(kc == k1 - 1)), (ones_r, sq_r), (p_r,))
                            act(OPF("activation", out=rq_h[:, rr, 0:T], in_=p_h[:, 0:T], func=AF.Sqrt, scale=1.0 / nfeat,
                                    bias=EPS), (p_r,), (rq_r,))
                            dve(OPF("reciprocal", out=rq_h[:, rr, 0:T], in_=rq_h[:, rr, 0:T]), (rq_r,), (rq_r,))
                            for kc in range(k0, k1):
                                g_h, g_r, gi = (GQh, GQr, kc) if rr == 0 else (GKVh, GKVr, kc - 3)
                                dve(OPF("scalar_tensor_tensor", out=cqn_h[:, kc, 0:T], in0=cq_h[:, kc, 0:T], scalar=g_h[:, gi:gi + 1],
                                    in1=rq_h[:, rr, 0:T], op0=ALU.mult, op1=ALU.mult),
                                    (cq_r, rq_r, g_r), (cqn_r,))
                        for h in range(H_A):
                            pm_h, pm_r = pz.next()
                            pp_h, pp_r = pz.next()
                            for (o_h, o_r, off) in ((pm_h, pm_r, h * 192), (pp_h, pp_r, h * 192 + 96)):
                                for kc in range(3):
                                    pe(OPF("matmul", o_h[0:96, 0:T], lhsT=wuq_h[:, kc, off:off + 96], rhs=cqn_h[:, kc, 0:T],
                                        start=(kc == 0), stop=(kc == 2)), (wuq_r, cqn_r), (o_r,))
                            act(OPF("activation", out=qas_h[0:64, h, 0:T], in_=pm_h[0:64, 0:T], func=AF.Copy), (pm_r,), (qas_r,))
                            rope_evac(pm_h, pm_r, pp_h, pp_r, rA_h, rA_r, 64, 96, qas_h[64:96, h, 0:T], qas_r)
                            pk_h, pk_r = pz.next()
                            for kc in range(2):
                                pe(OPF("matmul", pk_h[0:64, 0:T], lhsT=wkk_h[:, kc, h * 64:(h + 1) * 64], rhs=cqn_h[:, 3 + kc, 0:T],
                                    start=(kc == 0), stop=(kc == 1)), (wkk_r, cqn_r), (pk_r,))
                            act(OPF("activation", out=kas_h[:, h, 0:T], in_=pk_h[0:64, 0:T], func=AF.Copy), (pk_r,), (kas_r,))
                        st(dma(QA[b, :, :, p0:p0 + T].rearrange("h p t -> p h t"), qas_h[:, :, 0:T]), (qas_r,), (RQKV,))
                        st(dma(KA[b, :, :, p0:p0 + T].rearrange("h p t -> p h t"), kas_h[:, :, 0:T]), (kas_r,), (RQKV,))
                        for s in range(ns):
                            p_h, p_r = ptm.next()
                            for kc in range(2):
                                pe(OPF("matmul", p_h[:, :], lhsT=cqn_h[:, 3 + kc, s * 128:(s + 1) * 128], rhs=wkv_h[:, kc, :],
                                    start=(kc == 0), stop=(kc == 1)), (wkv_r, cqn_r), (p_r,))
                            dve(OPF("tensor_copy", out=vas_h[:, s, :, 0:64], in_=p_h[:, :].rearrange("p (h d) -> p h d", h=8)),
                                (p_r,), (vas_r,))
                        st(dma(VA[b, p0:p0 + T, :].rearrange("(s p) f -> p s f", p=128),
                               vas_h[:, 0:ns].rearrange("p s h d -> p s (h d)")), (vas_r,), (RQKV,))
                SC.barrier()
                SC.flush()
            if debug_stop == "A":
                break

            scaleA = float((NOPE_A + ROPE_A) ** -0.5)
            scaleB = float(DH_B ** -0.5)
            with ExitStack() as ph:
                vall = sb(ph, "vall", (128, NKT, 520), BF16, n=1)
                kT = sb(ph, "kT", (128, P), BF16, n=2)
                qT = sb(ph, "qT", (128, P), BF16, n=2)
                pTt = sb(ph, "pTt", (128, 2, 512), BF16, n=3)
                osb = sb(ph, "osb", (128, NKT, 1024), BF16, n=1)
                rl = sb(ph, "rl", (128, 8), F32, n=2)
                tt = sb(ph, "tt", (128, 128), F32, n=2)
                oo = sb(ph, "oo", (128, 128), F32, n=2)
                jk2 = sb(ph, "jk2", (128, 128), F32, n=1)
                ss2 = sb(ph, "ss2", (128, 2), F32, n=2)
                scp = ps(ph, "scp", (128, 2, 512), F32, n=2)
                accb = ps(ph, "accb", (128, 512), F32, n=4)
                LAMh, LAMr = LAM.h[0], LAM.r[0]
                GSBh, GSBr = GSB.h[0], GSB.r[0]
                for b in range(NB):
                    qblocks = ([] if last else [(0, CTX, CTX // 128)]) + [(CTX + i * 512, 512, NKT) for i in range(S // 512)]
                    o_h, o_r = osb.next()
                    v_h, v_r = vall.next()
                    ld(dma(v_h[:, :, :], VA[b].rearrange("(t p) f -> p t f", p=128)), (RQKV,), (v_r,))
                    for h in range(H_A):
                        k_h, k_r = kT.next()
                        q_h, q_r = qT.next()
                        ld(dma(k_h[0:64, :], KA[b, h]), (RQKV,), (k_r,))
                        ld(dma(k_h[64:96, :], KR[b]), (RQKV,), (k_r,))
                        ld(dma(q_h[0:96, :], QA[b, h]), (RQKV,), (q_r,))
                        units = []
                        for qi, (q0, nq, nkt) in enumerate(qblocks):
                            for kp in range(0, nkt, 2):
                                units.append((qi, q0, nq, nkt, kp))

                        def qk_mla(u):
                            qi, q0, nq, nkt, kp = u
                            s_h, s_r = scp.next()
                            for uu in range(2):
                                kt = kp + uu
                                pe(OPF("matmul", s_h[:, uu, 0:nq], lhsT=k_h[0:96, kt * 128:(kt + 1) * 128],
                                       rhs=q_h[0:96, q0:q0 + nq], start=True, stop=True), (k_r, q_r), (s_r,))
                            p_h, p_r = pTt.next()
                            act(OPF("activation", out=p_h[:, :, 0:nq], in_=s_h[:, :, 0:nq], func=AF.Exp, scale=scaleA),
                                (s_r,), (p_r,))
                            return p_h, p_r

                        def pv_mla(u, pp):
                            qi, q0, nq, nkt, kp = u
                            p_h, p_r = pp
                            nj = nq // 128
                            a_h, a_r = accb.h[qi % 4], accb.r[qi % 4]
                            for uu in range(2):
                                kt = kp + uu
                                for jj in range(nj):
                                    pe(OPF("matmul", a_h[:, jj * 65:(jj + 1) * 65], lhsT=p_h[:, uu, jj * 128:(jj + 1) * 128],
                                           rhs=v_h[:, kt, h * 65:(h + 1) * 65], start=(kt == 0 and jj == 0), stop=(kt == nkt - 1),
                                           skip_group_check=True), (p_r, v_r), (a_r,))
                            if kp + 2 >= nkt:
                                r_h, r_r = rl.next()
                                for jj in range(nj):
                                    qt = q0 // 128 + jj
                                    dve(OPF("reciprocal", out=r_h[:, jj:jj + 1], in_=a_h[:, jj * 65 + 64:jj * 65 + 65]), (a_r,), (r_r,))
                                    dve(OPF("tensor_scalar", out=o_h[:, qt, h * 64:(h + 1) * 64], in0=a_h[:, jj * 65:jj * 65 + 64],
                                            scalar1=r_h[:, jj:jj + 1], scalar2=None, op0=ALU.mult), (a_r, r_r), (o_r,))

                        prev = None
                        for u in units:
                            cur = qk_mla(u)
                            if prev is not None:
                                pv_mla(*prev)
                            prev = (u, cur)
                        pv_mla(*prev)
                    v_h, v_r = vall.next()
                    ld(dma(v_h[:, :, 0:516], VB[b].rearrange("(t p) f -> p t f", p=128)), (RQKV,), (v_r,))
                    for h in range(H_B):
                        k_h, k_r = kT.next()
                        q_h, q_r = qT.next()
                        ld(dma(k_h[:, :], KB[b, h]), (RQKV,), (k_r,))
                        ld(dma(q_h[:, :], QB[b, h]), (RQKV,), (q_r,))
                        units = []
                        for qi, (q0, nq, nkt) in enumerate(qblocks):
                            for kt in range(nkt):
                                units.append((qi, q0, nq, nkt, kt))
                        banks = [(accb.h[k_], accb.r[k_]) for k_ in range(4)]

                        def qk_diff(u):
                            qi, q0, nq, nkt, kt = u
                            s_h, s_r = scp.next()
                            for m in range(2):
                                pe(OPF("matmul", s_h[:, m, 0:nq], lhsT=k_h[m * 64:(m + 1) * 64, kt * 128:(kt + 1) * 128],
                                       rhs=q_h[m * 64:(m + 1) * 64, q0:q0 + nq], start=True, stop=True), (k_r, q_r), (s_r,))
                            p_h, p_r = pTt.next()
                            act(OPF("activation", out=p_h[:, :, 0:nq], in_=s_h[:, :, 0:nq], func=AF.Exp, scale=scaleB),
                                (s_r,), (p_r,))
                            return p_h, p_r

                        def pv_diff(u, pp):
                            qi, q0, nq, nkt, kt = u
                            p_h, p_r = pp
                            nj = nq // 128
                            for m in range(2):
                                for jj in range(nj):
                                    bk_h, bk_r = banks[m * 2 + jj // 2]
                                    off = (jj % 2) * 129
                                    pe(OPF("matmul", bk_h[:, off:off + 129], lhsT=p_h[:, m, jj * 128:(jj + 1) * 128],
                                           rhs=v_h[:, kt, h * 129:(h + 1) * 129], start=(kt == 0 and jj % 2 == 0),
                                           stop=(kt == nkt - 1), skip_group_check=True), (p_r, v_r), (bk_r,))
                            if kt != nkt - 1:
                                return
                            for jj in range(nj):
                                qt = q0 // 128 + jj
                                b1_h, b1_r = banks[jj // 2]
                                b2_h, b2_r = banks[2 + jj // 2]
                                off = (jj % 2) * 129
                                r_h, r_r = rl.next()
                                t_h, t_r = tt.next()
                                oo_h, oo_r = oo.next()
                                s2_h, s2_r = ss2.next()
                                jk_h, jk_r = jk2.next()
                                dve(OPF("reciprocal", out=r_h[:, 0:1], in_=b1_h[:, off + 128:off + 129]), (b1_r,), (r_r,))
                                dve(OPF("reciprocal", out=r_h[:, 1:2], in_=b2_h[:, off + 128:off + 129]), (b2_r,), (r_r,))
                                dve(OPF("tensor_tensor", out=r_h[:, 1:2], in0=r_h[:, 1:2], in1=LAMh[:, 0:1], op=ALU.mult),
                                    (r_r, LAMr), (r_r,))
                                dve(OPF("tensor_scalar", out=t_h[:], in0=b2_h[:, off:off + 128], scalar1=r_h[:, 1:2], scalar2=None,
                                        op0=ALU.mult), (b2_r, r_r), (t_r,))
                                dve(OPF("scalar_tensor_tensor", out=oo_h[:], in0=b1_h[:, off:off + 128], scalar=r_h[:, 0:1],
                                        in1=t_h[:], op0=ALU.mult, op1=ALU.subtract), (b1_r, r_r, t_r), (oo_r,))
                                dve(OPF("memset", s2_h[:], 0.0), (), (s2_r,))
                                act(OPF("activation", out=jk_h[:], in_=oo_h[:], func=AF.Square, accum_out=s2_h[:, 0:1]),
                                    (oo_r, s2_r), (jk_r, s2_r))
                                act(OPF("activation", out=s2_h[:, 1:2], in_=s2_h[:, 0:1], func=AF.Ln, scale=1.0 / V_B, bias=EPS),
                                    (s2_r,), (s2_r,))
                                act(OPF("activation", out=s2_h[:, 1:2], in_=s2_h[:, 1:2], func=AF.Exp, scale=-0.5),
                                    (s2_r,), (s2_r,))
                                dve(OPF("scalar_tensor_tensor", out=o_h[:, qt, 512 + h * 128:512 + (h + 1) * 128], in0=oo_h[:],
                                        scalar=s2_h[:, 1:2], in1=GSBh[:], op0=ALU.mult, op1=ALU.mult),
                                    (oo_r, s2_r, GSBr), (o_r,))

                        prev = None
                        for u in units:
                            cur = qk_diff(u)
                            if prev is not None:
                                pv_diff(*prev)
                            prev = (u, cur)
                        pv_diff(*prev)
                    qt0 = 0 if not last else CTX // 128
                    for (ta, tb) in ((qt0, NKT // 2), (NKT // 2, NKT)):
                        st(dma(OO[b, ta * 128:tb * 128, :].rearrange("(t p) f -> p t f", p=128), o_h[:, ta:tb, :]),
                           (o_r,), (ROO,))
                SC.barrier()
                SC.flush()
            if debug_stop == "B":
                break

            with ExitStack() as ph:
                ot = sb(ph, "ot", (128, 4, 1024), BF16, n=1)
                oT = sb(ph, "oT", (128, 8, 512), BF16, n=1)
                sg = sb(ph, "sg", (128, 2, 8, 512), BF16, n=1)
                mT = sb(ph, "mT", (128, 8, 512), BF16, n=1)
                t1 = sb(ph, "c1t1", (128, 512), F32, n=2)
                t2 = sb(ph, "c1t2", (128, 512), F32, n=2)
                xt = sb(ph, "c1xt", (128, 4, 1024), F32, n=1)
                xn = sb(ph, "c1xn", (128, 4, 1024), F32, n=1)
                ttm = sb(ph, "c1ttm", (128, 512), F32, n=2)
                nb_ss = sb(ph, "c1ss", (128, 4), F32, n=2)
                nb_junk = sb(ph, "c1junk", (128, 1024), BF16, n=1)
                nb_rstd = sb(ph, "c1rstd", (128, 4), F32, n=2)
                nb_xs = sb(ph, "c1xs", (128, 4, 1024), BF16, n=1)
                h2 = sb(ph, "c1h2", (128, 8, 512), BF16, n=2)
                zt_h, zt_r = sb(ph, "c1zt", (128, 8, 1), BF16).next()
                pT = ps(ph, "c1pT", (128, 1024), BF16, n=2)
                py = ps(ph, "c1py", (128, 512), F32, n=4)
                po = ps(ph, "c1po", (128, 512), F32, n=2)
                wa_h, wa_r = sb(ph, "c1wa", (128, 4, 1024), BF16).next()
                wb_h, wb_r = sb(ph, "c1wb", (128, 4, 1024), BF16).next()
                wo_h, wo_r = sb(ph, "c1wo", (128, 8, 1024), BF16).next()
                ld(dma(wa_h[:], wslab("wbra", l, 0, 1024)), (WR["wbra"][l],), (wa_r,))
                ld(dma(wb_h[:], wslab("wbrb", l, 0, 1024)), (WR["wbrb"][l],), (wb_r,))
                ld(dma(wo_h[:], wslab("wout", l, 0, 1024)), (WR["wout"][l],), (wo_r,))
                dve(OPF("memset", zt_h[:], 0.0), (), (zt_r,))
                nbufs = (nb_ss, nb_junk, nb_rstd, nb_xs, pT)
                GTBh, GTBr = GTB.h[0], GTB.r[0]
                for (b, is_ctx, j, pos0, length) in seqs:
                    if is_ctx and last:
                        continue
                    si = 0 if is_ctx else 1
                    for cpad in (0, length + 1):
                        st(OPF("dma_start", out=H2[b, si, :, :, cpad:cpad + 1].rearrange("c p t -> p c t"), in_=zt_h[:],
                               allow_slow_non_contiguous=True), (zt_r,), (RH2,))
                    for (t0, T) in tiles_of(length):
                        ns = T // 128
                        p0 = pos0 + t0
                        ot_h, ot_r = ot.next(); sg_h, sg_r = sg.next(); x_h, x_r = xt.next()
                        ld(dma(ot_h[:, 0:ns, :], OO[b, p0:p0 + T, :].rearrange("(s p) f -> p s f", p=128)), (ROO,), (ot_r,))
                        for g2 in range(2):
                            ld(dma(sg_h[:, g2, :, 0:T], GT[b, g2, :, :, p0:p0 + T].rearrange("c p t -> p c t")), (RQKV,), (sg_r,))
                        ld(dma(x_h[:, 0:ns, :], xsrc(b, is_ctx, t0, T).rearrange("(s p) d -> p s d", p=128)),
                           () if l == 0 else (RX2,), (x_r,))
                        oT_h, oT_r = oT.next()
                        for c2 in range(4):
                            p_h, p_r = pT.next()
                            for cc in range(2):
                                c = c2 * 2 + cc
                                for s in range(ns):
                                    pe(OPF("transpose", out=p_h[:, cc * 512 + s * 128: cc * 512 + (s + 1) * 128],
                                           in_=ot_h[:, s, c * 128:(c + 1) * 128], identity=ident_h[:]), (ot_r, ident_r), (p_r,))
                            for cc in range(2):
                                c = c2 * 2 + cc
                                if cc == 0:
                                    act(OPF("activation", out=oT_h[:, c, 0:T], in_=p_h[:, cc * 512:cc * 512 + T], func=AF.Copy),
                                        (p_r,), (oT_r,))
                                else:
                                    dve(OPF("tensor_copy", out=oT_h[:, c, 0:T], in_=p_h[:, cc * 512:cc * 512 + T]), (p_r,), (oT_r,))
                        mT_h, mT_r = mT.next()
                        for c in range(8):
                            pa_h, pa_r = py.next()
                            pb_h, pb_r = py.next()
                            for kc in range(4):
                                pe(OPF("matmul", pa_h[:, 0:T], lhsT=wa_h[:, kc, c * 128:(c + 1) * 128], rhs=oT_h[:, kc, 0:T],
                                       start=(kc == 0), stop=(kc == 3)), (wa_r, oT_r), (pa_r,))
                            for kc in range(4):
                                pe(OPF("matmul", pb_h[:, 0:T], lhsT=wb_h[:, kc, c * 128:(c + 1) * 128], rhs=oT_h[:, 4 + kc, 0:T],
                                       start=(kc == 0), stop=(kc == 3)), (wb_r, oT_r), (pb_r,))
                            a_h, a_r = t1.next()
                            b_h, b_r = t2.next()
                            dve(OPF("tensor_tensor", out=a_h[:, 0:T], in0=pa_h[:, 0:T], in1=sg_h[:, 0, c, 0:T], op=ALU.mult),
                                (pa_r, sg_r), (a_r,))
                            dve(OPF("tensor_tensor", out=b_h[:, 0:T], in0=pb_h[:, 0:T], in1=sg_h[:, 1, c, 0:T], op=ALU.mult),
                                (pb_r, sg_r), (b_r,))
                            dve(OPF("tensor_tensor", out=mT_h[:, c, 0:T], in0=a_h[:, 0:T], in1=b_h[:, 0:T], op=ALU.add),
                                (a_r, b_r), (mT_r,))
                        xn_h, xn_r = xn.next()
                        for s in range(ns):
                            for half in range(2):
                                p_h, p_r = po.next()
                                for kc in range(8):
                                    pe(OPF("matmul", p_h[:, :], lhsT=mT_h[:, kc, s * 128:(s + 1) * 128],
                                           rhs=wo_h[:, kc, half * 512:(half + 1) * 512], start=(kc == 0), stop=(kc == 7)),
                                       (wo_r, mT_r), (p_r,))
                                m_h, m_r = ttm.next()
                                dve(OPF("tensor_tensor", out=m_h[:], in0=p_h[:, :], in1=GTBh[:, j, 0, half * 512:(half + 1) * 512],
                                        op=ALU.mult), (p_r, GTBr), (m_r,))
                                dve(OPF("tensor_tensor", out=xn_h[:, s, half * 512:(half + 1) * 512], in0=m_h[:],
                                        in1=x_h[:, s, half * 512:(half + 1) * 512], op=ALU.add), (m_r, x_r), (xn_r,))
                        st(dma(X1[b, p0:p0 + T, :].rearrange("(s p) d -> p s d", p=128), xn_h[:, 0:ns, :]), (xn_r,), (RX1,))
                        h2_h, h2_r = h2.next()
                        norm_to_fm(xn_h, xn_r, ns, T, j, A2, B2, nbufs, h2_h, h2_r)
                        st(dma(H2[b, si, :, :, 1 + t0:1 + t0 + T].rearrange("c p t -> p c t"), h2_h[:, :, 0:T]), (h2_r,), (RH2,))
                SC.barrier()
                SC.flush()
            if debug_stop == "C1":
                break

            with ExitStack() as ph:
                h2 = sb(ph, "c2h2", (128, 8, 514), BF16, n=2)
                wsl = sb(ph, "c2wsl", (128, 8, 512), BF16, n=3)
                gT = sb(ph, "c2gT", (128, 22, 512), BF16, n=1)
                wd = sb(ph, "c2wd", (128, 22, 512), BF16, n=2)
                ua = sb(ph, "c2ua", (128, 512), F32, n=2)
                ub = sb(ph, "c2ub", (128, 512), F32, n=2)
                sa = sb(ph, "c2sa", (128, 512), F32, n=2)
                xm = sb(ph, "c2xm", (128, 4, 1024), F32, n=1)
                xo = sb(ph, "c2xo", (128, 4, 1024), F32, n=1)
                xf = sb(ph, "c2xf", (128, 4, 1024), F32, n=1)
                ttm = sb(ph, "c2ttm", (128, 512), F32, n=2)
                fss = sb(ph, "c2ss", (128, 4), F32, n=2)
                frs = sb(ph, "c2rs", (128, 4), F32, n=2)
                fjk = sb(ph, "c2jk", (128, 1024), BF16, n=1)
                pu = ps(ph, "c2pu", (128, 1024), F32, n=3)
                pd = ps(ph, "c2pd", (128, 512), F32, n=2)
                CWh, CWr, CBh, CBr = CW.h[0], CW.r[0], CB.h[0], CB.r[0]
                GTBh, GTBr = GTB.h[0], GTB.r[0]
                for (b, is_ctx, j, pos0, length) in seqs:
                    if is_ctx and last:
                        continue
                    si = 0 if is_ctx else 1
                    for (t0, T) in tiles_of(length):
                        ns = T // 128
                        p0 = pos0 + t0
                        h_h, h_r = h2.next()
                        x_h, x_r = xm.next()
                        ld(dma(h_h[:, :, 0:T + 2], H2[b, si, :, :, t0:t0 + T + 2].rearrange("c p t -> p c t")), (RH2,), (h_r,))
                        ld(dma(x_h[:, 0:ns, :], X1[b, p0:p0 + T, :].rearrange("(s p) d -> p s d", p=128)), (RX1,), (x_r,))
                        g_h, g_r = gT.next()
                        for i in range(22):
                            if i % 2 == 0:
                                w_h, w_r = wsl.next()
                                ld(dma(w_h[:], wslab("wup", l, (i // 2) * 512, (i // 2) * 512 + 512)), (WR["wup"][l],), (w_r,))
                            us = []
                            for which in range(2):
                                ch = 2 * i + which
                                off = ((i % 2) * 2 + which) * 128
                                p_h, p_r = pu.next()
                                for kc in range(8):
                                    pe(OPF("matmul", p_h[:, 0:T], lhsT=w_h[:, kc, off:off + 128], rhs=h_h[:, kc, 0:T],
                                           start=(kc == 0), stop=(kc == 7)), (w_r, h_r), (p_r,))
                                for kc in range(8):
                                    pe(OPF("matmul", p_h[:, T:T + 2], lhsT=w_h[:, kc, off:off + 128], rhs=h_h[:, kc, T:T + 2],
                                           start=(kc == 0), stop=(kc == 7)), (w_r, h_r), (p_r,))
                                u_h, u_r = (ua if which == 0 else ub).next()
                                act(OPF("activation", out=u_h[:, 0:T], in_=p_h[:, 1:T + 1], func=AF.Identity,
                                        scale=CWh[:, ch, 1:2], bias=CBh[:, ch:ch + 1]), (p_r, CWr, CBr), (u_r,))
                                dve(OPF("scalar_tensor_tensor", out=u_h[:, 0:T], in0=p_h[:, 0:T], scalar=CWh[:, ch, 0:1],
                                        in1=u_h[:, 0:T], op0=ALU.mult, op1=ALU.add), (p_r, CWr, u_r), (u_r,))
                                dve(OPF("scalar_tensor_tensor", out=u_h[:, 0:T], in0=p_h[:, 2:T + 2], scalar=CWh[:, ch, 2:3],
                                        in1=u_h[:, 0:T], op0=ALU.mult, op1=ALU.add), (p_r, CWr, u_r), (u_r,))
                                us.append((u_h, u_r))
                            s_h, s_r = sa.next()
                            act(OPF("activation", out=s_h[:, 0:T], in_=us[0][0][:, 0:T], func=AF.Silu), (us[0][1],), (s_r,))
                            dve(OPF("tensor_tensor", out=g_h[:, i, 0:T], in0=s_h[:, 0:T], in1=us[1][0][:, 0:T], op=ALU.mult),
                                (s_r, us[1][1]), (g_r,))
                        xo_h, xo_r = xo.next()
                        for half in range(2):
                            d_h, d_r = wd.next()
                            ld(dma(d_h[:], wslab("wdown", l, half * 512, half * 512 + 512)), (WR["wdown"][l],), (d_r,))
                            for s in range(ns):
                                p_h, p_r = pd.next()
                                for kc in range(22):
                                    pe(OPF("matmul", p_h[:, :], lhsT=g_h[:, kc, s * 128:(s + 1) * 128], rhs=d_h[:, kc, :],
                                           start=(kc == 0), stop=(kc == 21)), (d_r, g_r), (p_r,))
                                m_h, m_r = ttm.next()
                                dve(OPF("tensor_tensor", out=m_h[:], in0=p_h[:, :], in1=GTBh[:, j, 1, half * 512:(half + 1) * 512],
                                        op=ALU.mult), (p_r, GTBr), (m_r,))
                                dve(OPF("tensor_tensor", out=xo_h[:, s, half * 512:(half + 1) * 512], in0=m_h[:],
                                        in1=x_h[:, s, half * 512:(half + 1) * 512], op=ALU.add), (m_r, x_r), (xo_r,))
                        if not last:
                            st(dma(X2[b, p0:p0 + T, :].rearrange("(s p) d -> p s d", p=128), xo_h[:, 0:ns, :]), (xo_r,), (RX2,))
                        else:
                            ss_h, ss_r = fss.next(); rs_h, rs_r = frs.next(); jk_h, jk_r = fjk.next(); xf_h, xf_r = xf.next()
                            dve(OPF("memset", ss_h[:], 0.0), (), (ss_r,))
                            for s in range(ns):
                                act(OPF("activation", out=jk_h[:], in_=xo_h[:, s, :], func=AF.Square, accum_out=ss_h[:, s:s + 1]),
                                    (xo_r, ss_r), (jk_r, ss_r))
                            act(OPF("activation", out=rs_h[:, 0:ns], in_=ss_h[:, 0:ns], func=AF.Sqrt, scale=1.0 / D, bias=EPS),
                                (ss_r,), (rs_r,))
                            dve(OPF("reciprocal", out=rs_h[:, 0:ns], in_=rs_h[:, 0:ns]), (rs_r,), (rs_r,))
                            for s in range(ns):
                                dve(OPF("scalar_tensor_tensor", out=xf_h[:, s, :], in0=xo_h[:, s, :], scalar=rs_h[:, s:s + 1],
                                        in1=gfin_h[:], op0=ALU.mult, op1=ALU.mult), (xo_r, rs_r, gfin_r), (xf_r,))
                            st(dma(out[b, t0:t0 + T, :].rearrange("(s p) d -> p s d", p=128), xf_h[:, 0:ns, :]), (xf_r,), ())
                SC.barrier()
                SC.flush()
            if debug_stop == "C2" + str(l):
                break
    return nc


_CACHE = {}


def kernel(**inputs):
    S = inputs["x"].shape[1]
    CTX = inputs["ctx"].shape[1]
    B = inputs["x"].shape[0]
    ncores = 8
    nb = B // ncores
    nc = build_program(S, CTX, nb)
    shared = prep_shared(inputs, S, CTX)
    in_maps = []
    for c in range(ncores):
        m = dict(shared)
        m.update(prep_core(inputs, c, nb))
        in_maps.append(m)
    res = run_bass_kernel_spmd(nc, in_maps, core_ids=list(range(ncores)))
    return np.concatenate([np.asarray(r["out"]) for r in res.results], axis=0).astype(np.float32)
```

```python
import math
from contextlib import ExitStack

import numpy as np
import concourse.bass as bass
import concourse.mybir as mybir
from concourse.bass_utils import run_bass_kernel_spmd

F32, BF16 = mybir.dt.float32, mybir.dt.bfloat16
AF = mybir.ActivationFunctionType
ALU = mybir.AluOpType

D = 1024
DEPTH = 2
GRID_W = 64
EPS = 1e-6
ROPE_BASE = 10000.0
H_A, Q_LORA, KV_LORA, NOPE_A, ROPE_A, V_A = 8, 384, 256, 64, 32, 64
H_B, DH_B, V_B = 4, 64, 128
D_FF = 2816
NIN = 5632
NUQ = 1536
NCH_UP = 44


class Res:
    __slots__ = ("w", "rs")

    def __init__(self):
        self.w = None
        self.rs = []


class Op:
    __slots__ = ("eng", "fn", "dma", "deps", "inc", "sem", "val", "done")


ENGS = ("pe", "act", "dve", "pool", "sp")


class Sched:
    def __init__(self, nc, esem, rings):
        self.nc = nc
        self.esem = esem
        self.rings = rings
        self.ring_cnt = {e: [0] * len(r) for e, r in rings.items()}
        self.ring_pos = {e: 0 for e in rings}
        self.ticks = {e: 0 for e in ENGS}
        self.known = {e: {} for e in ENGS}
        self.handles = {}
        for e, (i, h) in esem.items():
            self.handles[i] = h
        for e, r in rings.items():
            for i, h in r:
                self.handles[i] = h
        self.ops = {e: [] for e in ENGS}
        self.dma_since_barrier = []
        self.n_ops = 0

    def add(self, eng, fn, reads=(), writes=(), dma=False):
        o = Op()
        o.eng, o.fn, o.dma, o.inc, o.sem, o.val = eng, fn, dma, dma, None, None
        o.done = False
        deps = []
        for r in reads:
            p = r.w
            if p is not None and (p.dma or dma or p.eng != eng or eng != "pe"):
                deps.append(p)
        for w in writes:
            p = w.w
            if p is not None and (p.dma or dma or p.eng != eng):
                deps.append(p)
            for p in w.rs:
                if p.dma or dma or p.eng != eng:
                    deps.append(p)
        for r in reads:
            rs = r.rs
            if not dma:
                for i in range(len(rs)):
                    if (not rs[i].dma) and rs[i].eng == eng:
                        rs[i] = o
                        break
                else:
                    rs.append(o)
            else:
                rs.append(o)
        for w in writes:
            w.w = o
            w.rs = []
        deps = [p for p in deps if not p.done]
        for p in deps:
            p.inc = True
        o.deps = deps
        self.ops[eng].append(o)
        if dma:
            self.dma_since_barrier.append(o)
        self.n_ops += 1
        return o

    def barrier(self):
        last = {e: (self.ops[e][-1] if self.ops[e] else None) for e in ENGS}
        dmas = list(self.dma_since_barrier)
        for e in ENGS:
            o = Op()
            o.eng, o.fn, o.dma, o.inc, o.sem, o.val = e, None, False, False, None, None
            o.done = False
            o.deps = [p for e2, p in last.items() if p is not None and e2 != e and p.fn is not None] + dmas
            for p in o.deps:
                p.inc = True
            self.ops[e].append(o)
        self.dma_since_barrier = []

    def flush(self):
        for e in ENGS:
            for o in self.ops[e]:
                if o.dma:
                    ring = self.rings[e]
                    k = self.ring_pos[e]
                    self.ring_pos[e] = (k + 1) % len(ring)
                    self.ring_cnt[e][k] += 1
                    o.sem = ring[k][0]
                    o.val = 16 * self.ring_cnt[e][k]
                elif o.inc and o.fn is not None:
                    self.ticks[e] += 1
                    o.sem = self.esem[e][0]
                    o.val = self.ticks[e]
        with self.nc.Block() as block:
            def run(e):
                def body(eng):
                    known = self.known[e]
                    for o in self.ops[e]:
                        for p in o.deps:
                            if known.get(p.sem, 0) < p.val:
                                known[p.sem] = p.val
                                eng.wait_ge(self.handles[p.sem], p.val)
                        if o.dma and o.val > 16 and known.get(o.sem, 0) < o.val - 16:
                            known[o.sem] = o.val - 16
                            eng.wait_ge(self.handles[o.sem], o.val - 16)
                        if o.fn is not None:
                            ins = o.fn(eng)
                            if o.dma:
                                ins.then_inc(self.handles[o.sem], 16)
                            elif o.inc:
                                ins.then_inc(self.handles[o.sem], 1)
                return body
            block.tensor(run("pe"))
            block.scalar(run("act"))
            block.vector(run("dve"))
            block.gpsimd(run("pool"))
            block.sync(run("sp"))
        for e in ENGS:
            for o in self.ops[e]:
                o.done = True
        self.ops = {e: [] for e in ENGS}


class Buf:
    def __init__(self, handles):
        self.h = handles
        self.r = [Res() for _ in handles]
        self.i = -1

    def next(self):
        self.i = (self.i + 1) % len(self.h)
        return self.h[self.i], self.r[self.i]

    def cur(self):
        return self.h[self.i], self.r[self.i]


def _rope_tables(S, CTX):
    rows = np.repeat(np.arange(S // GRID_W), GRID_W).astype(np.float32)
    cols = np.tile(np.arange(GRID_W), S // GRID_W).astype(np.float32)

    def tab(dr):
        q = dr // 4
        freqs = (ROPE_BASE ** (-np.arange(q, dtype=np.float32) / q)).astype(np.float32)
        ang = np.concatenate([rows[:, None] * freqs, cols[:, None] * freqs], axis=-1)
        cos, sin = np.cos(ang).astype(np.float32), np.sin(ang).astype(np.float32)
        cr, cc, sr, sc = cos[:, :q], cos[:, q:], sin[:, :q], sin[:, q:]
        c = np.concatenate([cr, cr, cc, cc], axis=-1)
        s = np.concatenate([-sr, sr, -sc, sc], axis=-1)
        c = np.concatenate([np.ones((CTX, dr), np.float32), c], axis=0)
        s = np.concatenate([np.zeros((CTX, dr), np.float32), s], axis=0)
        return np.ascontiguousarray(c.T), np.ascontiguousarray(s.T)

    ca, sa = tab(ROPE_A)
    cb, sb = tab(DH_B)
    ropeA = np.zeros((2, 96, CTX + S), np.float32)
    ropeA[0, 64:96], ropeA[1, 64:96] = ca, sa
    ropeB = np.stack([np.concatenate([cb, cb], 0), np.concatenate([sb, sb], 0)], 0)
    return ropeA, ropeB


def _partner(dr):
    q = dr // 4
    return np.concatenate([np.arange(q, 2 * q), np.arange(0, q), np.arange(3 * q, 4 * q), np.arange(2 * q, 3 * q)])


def _win_index():
    Z = 4256
    idx = list(range(0, 640))
    pa = _partner(ROPE_A)
    idx += [Z] * 64 + list(range(640, 672)) + [Z] * 32
    idx += [Z] * 64 + list(640 + pa) + [Z] * 32
    idx += [Z] * 128
    pb = _partner(DH_B)
    for base in (672, 1184):
        for h in range(H_B):
            b0 = base + h * 128
            idx += list(range(b0, b0 + 128))
            idx += list(b0 + pb) + list(b0 + 64 + pb)
    idx += list(range(2208, 4256))
    idx += list(range(1696, 2208))
    assert len(idx) == NIN
    return np.array(idx)


def _wuq_index():
    Z = 768
    pa = _partner(ROPE_A)
    idx = []
    for h in range(H_A):
        b0 = h * 96
        idx += list(range(b0, b0 + 96))
        idx += [Z] * 64 + list(b0 + 64 + pa)
    return np.array(idx)


def _wup_index():
    idx = []
    for i in range(D_FF // 128):
        idx += list(range(i * 128, (i + 1) * 128)) + list(range(D_FF + i * 128, D_FF + (i + 1) * 128))
    return np.array(idx)


def _fm(v, nch):
    return np.ascontiguousarray(np.asarray(v, np.float32).reshape(nch, 128).T)


def prep_shared(inp, S, CTX):
    f = lambda a: np.asarray(a, np.float32)
    L = DEPTH
    zc = lambda w: np.concatenate([w, np.zeros(w.shape[:-1] + (1,), np.float32)], axis=-1)
    wi, uq, up = _win_index(), _wuq_index(), _wup_index()
    sh = {}
    sh["wada"] = np.ascontiguousarray(f(inp["w_ada"]))
    sh["win"] = np.ascontiguousarray(zc(f(inp["w_in"]))[:, :, wi])
    sh["wuq"] = np.ascontiguousarray(zc(f(inp["w_uq"]))[:, :, uq])
    wukv = f(inp["w_ukv"]).reshape(L, KV_LORA, H_A, 128)
    sh["wukvk"] = np.ascontiguousarray(wukv[..., :64].reshape(L, KV_LORA, 512))
    sh["wukvv"] = np.ascontiguousarray(wukv[..., 64:].reshape(L, KV_LORA, 512))
    sh["wbra"] = np.ascontiguousarray(f(inp["w_br_a"]))
    sh["wbrb"] = np.ascontiguousarray(f(inp["w_br_b"]))
    sh["wout"] = np.ascontiguousarray(f(inp["w_out"]))
    sh["wup"] = np.ascontiguousarray(f(inp["w_up"])[:, :, up])
    sh["wdown"] = np.ascontiguousarray(f(inp["w_down"]))
    bada = f(inp["b_ada"])
    sh["badafm"] = np.stack([_fm(bada[l], 48) for l in range(L)])
    sh["badabc"] = np.ascontiguousarray(np.broadcast_to(
        np.stack([bada[:, 2048:3072], bada[:, 5120:6144]], 1)[:, None], (L, 128, 2, 1024)))
    sh["gmix"] = np.stack([_fm(f(inp["g_mix"])[l], 8) for l in range(L)])
    sh["gffn"] = np.stack([_fm(f(inp["g_ffn"])[l], 8) for l in range(L)])
    sh["gq"] = np.stack([_fm(f(inp["g_q"])[l], 3) for l in range(L)])
    sh["gkv"] = np.stack([_fm(f(inp["g_kv"])[l], 2) for l in range(L)])
    lam = np.stack([f(inp["lam_q1"]), f(inp["lam_k1"]), f(inp["lam_q2"]), f(inp["lam_k2"])], 1)
    sh["lamv"] = np.ascontiguousarray(np.broadcast_to(lam[:, None], (L, 128, 4, 64)))
    sh["gsubbc"] = np.ascontiguousarray(np.broadcast_to(f(inp["g_sub"])[:, None], (L, 128, 128)))
    cw = f(inp["conv_w"])[:, :, up]
    sh["convw"] = np.ascontiguousarray(cw.reshape(L, 3, NCH_UP, 128).transpose(0, 3, 2, 1))
    sh["convb"] = np.stack([_fm(f(inp["conv_b"])[l][up], NCH_UP) for l in range(L)])
    sh["gfinbc"] = np.ascontiguousarray(np.broadcast_to(f(inp["g_final"])[None], (128, 1024)))
    ra, rb = _rope_tables(S, CTX)
    sh["ropeA"], sh["ropeB"] = ra, rb
    sh["ident"] = np.eye(128, dtype=np.float32)
    return sh


def prep_core(inp, core, nb):
    f = lambda a: np.asarray(a, np.float32)
    b0 = core * nb
    x = np.ascontiguousarray(f(inp["x"])[b0:b0 + nb])
    ctx = np.ascontiguousarray(f(inp["ctx"])[b0:b0 + nb])
    cs = [f(inp["c"])[b0 + j] for j in range(nb)] + [f(inp["c_ctx"])]
    cT = np.ascontiguousarray(np.stack([c.reshape(8, 128).T for c in cs], axis=-1))
    return {"x": x, "ctx": ctx, "cT": cT}


def build_program(S, CTX, NB, depth=DEPTH, debug_stop=None):
    P = CTX + S
    NKT = P // 128
    NJ = NB + 1
    nc = bass.Bass("TRN2", target_bir_lowering=False)
    dbg = debug_stop is not None
    skind = "ExternalOutput" if dbg else "Internal"

    def din(name, shape):
        return nc.dram_tensor(name, list(shape), F32, kind="ExternalInput").ap()

    def dscr(name, shape, dt=BF16, kind=None):
        return nc.dram_tensor(name, list(shape), dt, kind=kind or skind).ap()

    L = depth
    I = {}
    I["x"] = din("x", (NB, S, D))
    I["ctx"] = din("ctx", (NB, CTX, D))
    I["cT"] = din("cT", (128, 8, NJ))
    wshapes = {"wada": (DEPTH, D, 6 * D), "win": (DEPTH, D, NIN), "wuq": (DEPTH, Q_LORA, NUQ),
               "wukvk": (DEPTH, KV_LORA, 512), "wukvv": (DEPTH, KV_LORA, 512),
               "wbra": (DEPTH, 512, D), "wbrb": (DEPTH, 512, D), "wout": (DEPTH, D, D),
               "wup": (DEPTH, D, 2 * D_FF), "wdown": (DEPTH, D_FF, D)}
    for k, shp in wshapes.items():
        I[k] = din(k, shp)
    for k, shp in {"badafm": (DEPTH, 128, 48), "badabc": (DEPTH, 128, 2, 1024), "gmix": (DEPTH, 128, 8),
                   "gffn": (DEPTH, 128, 8), "gq": (DEPTH, 128, 3), "gkv": (DEPTH, 128, 2),
                   "lamv": (DEPTH, 128, 4, 64), "gsubbc": (DEPTH, 128, 128),
                   "convw": (DEPTH, 128, NCH_UP, 3), "convb": (DEPTH, 128, NCH_UP),
                   "gfinbc": (128, 1024), "ropeA": (2, 96, P), "ropeB": (2, 128, P),
                   "ident": (128, 128)}.items():
        I[k] = din(k, shp)
    out = nc.dram_tensor("out", [NB, S, D], F32, kind="ExternalOutput").ap()

    W = {k: dscr("b_" + k, shp, kind="Internal") for k, shp in wshapes.items()}
    WR = {k: [Res() for _ in range(DEPTH)] for k in wshapes}
    X1 = dscr("X1", (NB, P, D), F32)
    X2 = dscr("X2", (NB, P, D), F32)
    QA = dscr("QA", (NB, H_A, 96, P))
    KA = dscr("KA", (NB, H_A, 64, P))
    KR = dscr("KR", (NB, 32, P))
    VA = dscr("VA", (NB, P, H_A * 65))
    QB = dscr("QB", (NB, H_B, 128, P))
    KB = dscr("KB", (NB, H_B, 128, P))
    VB = dscr("VB", (NB, P, H_B * 129))
    GT = dscr("GT", (NB, 2, 8, 128, P))
    OO = dscr("OO", (NB, P, D))
    H2 = dscr("H2", (NB, 2, 8, 128, P + 2))
    RX1, RX2, RQKV, ROO, RH2 = Res(), Res(), Res(), Res(), Res()

    es = ExitStack()
    with es:
        nsem = {"n": 0}

        def sem(name):
            h = es.enter_context(nc.semaphore(name))
            nsem["n"] += 1
            return (nsem["n"], h)

        esem = {e: sem("e_" + e) for e in ("pe", "act", "dve", "pool")}
        rings = {"sp": [sem(f"r_sp{i}") for i in range(12)], "pool": [sem(f"r_pl{i}") for i in range(12)],
                 "act": [sem(f"r_ac{i}") for i in range(2)]}
        esem["sp"] = esem["pool"]
        SC = Sched(nc, esem, rings)

        uid = {"n": 0}

        def sb(stack, name, shape, dt=F32, n=1):
            uid["n"] += 1
            return Buf([stack.enter_context(nc.sbuf_tensor(f"{name}_{uid['n']}_{i}", list(shape), dt)) for i in range(n)])

        def ps(stack, name, shape, dt=F32, n=1):
            uid["n"] += 1
            return Buf([stack.enter_context(nc.psum_tensor(f"{name}_{uid['n']}_{i}", list(shape), dt)) for i in range(n)])

        pe = lambda fn, r=(), w=(): SC.add("pe", fn, r, w)
        act = lambda fn, r=(), w=(): SC.add("act", fn, r, w)
        dve = lambda fn, r=(), w=(): SC.add("dve", fn, r, w)
        pool = lambda fn, r=(), w=(): SC.add("pool", fn, r, w)
        ld = lambda fn, r=(), w=(): SC.add("sp", fn, r, w, dma=True)
        st = lambda fn, r=(), w=(): SC.add("pool", fn, r, w, dma=True)

        def OPF(m, *a, **k):
            return lambda e: getattr(e, m)(*a, **k)

        def dma(o, i):
            return OPF("dma_start", out=o, in_=i)

        ident_h, ident_r = sb(es, "ident", (128, 128), BF16).next()
        ones_h, ones_r = sb(es, "ones", (128, 128), BF16).next()
        onesf_h, onesf_r = sb(es, "onesf", (128, 128), F32).next()
        gfin_h, gfin_r = sb(es, "gfin", (128, 1024), F32).next()
        dve(OPF("memset", ones_h[:], 1.0), (), (ones_r,))
        dve(OPF("memset", onesf_h[:], 1.0), (), (onesf_r,))
        ld(dma(gfin_h[:], I["gfinbc"]), (), (gfin_r,))
        A1 = sb(es, "A1", (128, NJ, 8)); B1 = sb(es, "B1", (128, NJ, 8))
        A2 = sb(es, "A2", (128, NJ, 8)); B2 = sb(es, "B2", (128, NJ, 8))
        GTB = sb(es, "GTB", (128, NJ, 2, 1024))
        LAM = sb(es, "LAM", (128, 4))
        GSB = sb(es, "GSB", (128, 128))
        for b_ in (A1, B1, A2, B2, GTB, LAM, GSB):
            b_.next()
        GQ = sb(es, "GQ", (128, 3)); GKV = sb(es, "GKV", (128, 2)); GQ.next(); GKV.next()
        CW = sb(es, "CW", (128, NCH_UP, 3)); CB = sb(es, "CB", (128, NCH_UP)); CW.next(); CB.next()

        with ExitStack() as ph:
            w32 = sb(ph, "w32", (128, 6144), F32, n=2)
            w16 = sb(ph, "w16", (128, 6144), BF16, n=2)
            i32_h, i32_r = sb(ph, "i32", (128, 128), F32).next()
            ld(dma(i32_h[:], I["ident"]), (), (i32_r,))
            dve(OPF("tensor_copy", out=ident_h[:], in_=i32_h[:]), (i32_r,), (ident_r,))
            k_i = 0
            for l in range(depth if debug_stop != "W0" else 0):
                for k, shp in wshapes.items():
                    rows, ncol = shp[1], shp[2]
                    for r0 in range(0, rows, 128):
                        a_h, a_r = w32.next()
                        b_h, b_r = w16.next()
                        ld(dma(a_h[:, 0:ncol], I[k][l, r0:r0 + 128, :]), (), (a_r,))
                        hc = ncol // 2
                        for (c0, c1) in ((0, hc), (hc, ncol)):
                            k_i += 1
                            if k_i % 2 == 0:
                                dve(OPF("tensor_copy", out=b_h[:, c0:c1], in_=a_h[:, c0:c1]), (a_r,), (b_r,))
                            else:
                                act(OPF("activation", out=b_h[:, c0:c1], in_=a_h[:, c0:c1], func=AF.Copy), (a_r,), (b_r,))
                        st(dma(W[k][l, r0:r0 + 128, :], b_h[:, 0:ncol]), (b_r,), (WR[k][l],))
            SC.barrier()
            SC.flush()

        if debug_stop in ("W", "W0"):
            return nc

        def wslab(k, l, c0, c1):
            return W[k][l].rearrange("(k p) n -> p k n", p=128)[:, :, c0:c1]

        seqs = []
        for b in range(NB):
            seqs.append((b, True, NB, 0, CTX))
            seqs.append((b, False, b, CTX, S))

        def tiles_of(length, T=512):
            return [(t0, min(T, length - t0)) for t0 in range(0, length, T)]

        for l in range(depth):
            last = l == depth - 1
            lam_init = 0.8 - 0.6 * math.exp(-0.3 * l)
            Xin = None if l == 0 else (X2)

            def xsrc(b, is_ctx, t0, T):
                if l == 0:
                    src = I["ctx"][b] if is_ctx else I["x"][b]
                    return src[t0:t0 + T, :]
                p0 = t0 if is_ctx else CTX + t0
                return X2[b, p0:p0 + T, :]

            with ExitStack() as ph:
                cT = sb(ph, "cT", (128, 8, NJ)); cT_h, cT_r = cT.next()
                scf_h, scf_r = sb(ph, "scf", (128, 8, NJ)).next()
                scb_h, scb_r = sb(ph, "scb", (128, 8, NJ), BF16).next()
                rep_h, rep_r = sb(ph, "rep", (128, 8, NJ, 128), BF16).next()
                wsl = sb(ph, "mwsl", (128, 8, 512), BF16, n=3)
                modT_h, modT_r = sb(ph, "modT", (128, 48, NJ)).next()
                bfm_h, bfm_r = sb(ph, "bfm", (128, 48)).next()
                bbc_h, bbc_r = sb(ph, "bbc", (128, 2, 1024)).next()
                gm_h, gm_r = sb(ph, "gm", (128, 8)).next()
                gf_h, gf_r = sb(ph, "gf", (128, 8)).next()
                lv_h, lv_r = sb(ph, "lv", (128, 4, 64)).next()
                lt_h, lt_r = sb(ph, "lt", (128, 2, 64)).next()
                tmp_h, tmp_r = sb(ph, "mtmp", (128, 8)).next()
                pm = ps(ph, "pm", (128, 512), n=2)
                pg = ps(ph, "pg", (128, 512), n=2)
                ld(dma(cT_h[:], I["cT"]), (), (cT_r,))
                ld(dma(bfm_h[:], I["badafm"][l]), (), (bfm_r,))
                ld(dma(bbc_h[:], I["badabc"][l]), (), (bbc_r,))
                ld(dma(gm_h[:], I["gmix"][l]), (), (gm_r,))
                ld(dma(gf_h[:], I["gffn"][l]), (), (gf_r,))
                ld(dma(lv_h[:], I["lamv"][l]), (), (lv_r,))
                ld(dma(GSB.h[0][:], I["gsubbc"][l]), (), (GSB.r[0],))
                ld(dma(GQ.h[0][:], I["gq"][l]), (), (GQ.r[0],))
                ld(dma(GKV.h[0][:], I["gkv"][l]), (), (GKV.r[0],))
                ld(dma(CW.h[0][:], I["convw"][l]), (), (CW.r[0],))
                ld(dma(CB.h[0][:], I["convb"][l]), (), (CB.r[0],))
                act(OPF("activation", out=scf_h[:], in_=cT_h[:], func=AF.Silu), (cT_r,), (scf_r,))
                dve(OPF("tensor_copy", out=scb_h[:], in_=scf_h[:]), (scf_r,), (scb_r,))
                for kc in range(8):
                    for j in range(NJ):
                        dve(OPF("tensor_scalar", out=rep_h[:, kc, j, :], in0=onesf_h[:], scalar1=scf_h[:, kc, j:j + 1], scalar2=None,
                            op0=ALU.mult), (scf_r, onesf_r), (rep_r,))
                dve(OPF("tensor_scalar", out=GSB.h[0][:], in0=GSB.h[0][:], scalar1=float(1.0 - lam_init),
                                              scalar2=None, op0=ALU.mult), (GSB.r[0],), (GSB.r[0],))
                LAMh, LAMr = LAM.h[0], LAM.r[0]
                for m in range(2):
                    dve(OPF("tensor_tensor", out=lt_h[:, m, :], in0=lv_h[:, 2 * m, :],
                                                       in1=lv_h[:, 2 * m + 1, :], op=ALU.mult), (lv_r,), (lt_r,))
                    dve(OPF("reduce_sum", out=LAMh[:, 2 + m:3 + m], in_=lt_h[:, m, :],
                                                    axis=mybir.AxisListType.X), (lt_r,), (LAMr,))
                act(OPF("activation", out=LAMh[:, 2:4], in_=LAMh[:, 2:4], func=AF.Exp), (LAMr,), (LAMr,))
                dve(OPF("tensor_tensor", out=LAMh[:, 0:1], in0=LAMh[:, 2:3], in1=LAMh[:, 3:4],
                                              op=ALU.subtract), (LAMr,), (LAMr,))
                dve(OPF("tensor_scalar", out=LAMh[:, 0:1], in0=LAMh[:, 0:1], scalar1=float(lam_init),
                                              scalar2=None, op0=ALU.add), (LAMr,), (LAMr,))
                for sl in range(12):
                    w_h, w_r = wsl.next()
                    ld(dma(w_h[:], wslab("wada", l, sl * 512, sl * 512 + 512)), (WR["wada"][l],), (w_r,))
                    for cc in range(4):
                        ch = sl * 4 + cc
                        p_h, p_r = pm.next()
                        for kc in range(8):
                            pe(OPF("matmul", p_h[:, 0:NJ], lhsT=w_h[:, kc, cc * 128:(cc + 1) * 128], rhs=scb_h[:, kc, :],
                                start=(kc == 0), stop=(kc == 7)), (w_r, scb_r), (p_r,))
                        dve(OPF("tensor_scalar", out=modT_h[:, ch, :], in0=p_h[:, 0:NJ], scalar1=bfm_h[:, ch:ch + 1], scalar2=None,
                            op0=ALU.add), (p_r, bfm_r), (modT_r,))
                    if sl in (4, 5, 10, 11):
                        which, half = (0 if sl < 6 else 1), sl % 2
                        for j in range(NJ):
                            g_h, g_r = pg.next()
                            for kc in range(8):
                                pe(OPF("matmul", g_h[:], lhsT=rep_h[:, kc, j, :], rhs=w_h[:, kc, :],
                                    start=(kc == 0), stop=(kc == 7)), (w_r, rep_r), (g_r,))
                            dve(OPF("tensor_tensor", out=GTB.h[0][:, j, which, half * 512:(half + 1) * 512], in0=g_h[:],
                                in1=bbc_h[:, which, half * 512:(half + 1) * 512], op=ALU.add),
                                (g_r, bbc_r), (GTB.r[0],))
                for j in range(NJ):
                    for (Ab, Bb, g_h, g_r, sc0, sh0) in ((A1, B1, gm_h, gm_r, 8, 0), (A2, B2, gf_h, gf_r, 32, 24)):
                        dve(OPF("tensor_scalar", out=tmp_h[:], in0=modT_h[:, sc0:sc0 + 8, j], scalar1=1.0, scalar2=None, op0=ALU.add),
                            (modT_r,), (tmp_r,))
                        dve(OPF("tensor_tensor", out=Ab.h[0][:, j, :], in0=tmp_h[:], in1=g_h[:], op=ALU.mult),
                            (tmp_r, g_r), (Ab.r[0],))
                        dve(OPF("tensor_copy", out=Bb.h[0][:, j, :], in_=modT_h[:, sh0:sh0 + 8, j]), (modT_r,), (Bb.r[0],))
                SC.barrier()
                SC.flush()

            def norm_a(x_h, x_r, ns, bufs):
                ss, junk, rstd, xs, pT = bufs
                ss_h, ss_r = ss.next()
                jk_h, jk_r = junk.next()
                rs_h, rs_r = rstd.next()
                xs_h, xs_r = xs.next()
                dve(OPF("memset", ss_h[:], 0.0), (), (ss_r,))
                for s in range(ns):
                    act(OPF("activation", out=jk_h[:], in_=x_h[:, s, :], func=AF.Square,
                                                    accum_out=ss_h[:, s:s + 1]), (x_r, ss_r), (jk_r, ss_r))
                act(OPF("activation", out=rs_h[:, 0:ns], in_=ss_h[:, 0:ns], func=AF.Sqrt, scale=1.0 / D, bias=EPS),
                    (ss_r,), (rs_r,))
                dve(OPF("reciprocal", out=rs_h[:, 0:ns], in_=rs_h[:, 0:ns]), (rs_r,), (rs_r,))
                for s in range(ns):
                    dve(OPF("tensor_scalar", out=xs_h[:, s, :], in0=x_h[:, s, :], scalar1=rs_h[:, s:s + 1],
                                                       scalar2=None, op0=ALU.mult), (x_r, rs_r), (xs_r,))
                return xs_h, xs_r

            def norm_b(xs_h, xs_r, ns, T, j, Ab, Bb, bufs, hT_h, hT_r, col0=0):
                pT = bufs[4]
                for c2 in range(4):
                    p_h, p_r = pT.next()
                    for cc in range(2):
                        c = c2 * 2 + cc
                        for s in range(ns):
                            pe(OPF("transpose", out=p_h[:, cc * 512 + s * 128: cc * 512 + (s + 1) * 128],
                                in_=xs_h[:, s, c * 128:(c + 1) * 128], identity=ident_h[:]),
                                (xs_r, ident_r), (p_r,))
                    for cc in range(2):
                        c = c2 * 2 + cc
                        act(OPF("activation", out=hT_h[:, c, col0:col0 + T], in_=p_h[:, cc * 512:cc * 512 + T], func=AF.Identity,
                            scale=Ab.h[0][:, j, c:c + 1], bias=Bb.h[0][:, j, c:c + 1]),
                            (p_r, Ab.r[0], Bb.r[0]), (hT_r,))

            def norm_to_fm(x_h, x_r, ns, T, j, Ab, Bb, bufs, hT_h, hT_r, col0=0):
                xs_h, xs_r = norm_a(x_h, x_r, ns, bufs)
                norm_b(xs_h, xs_r, ns, T, j, Ab, Bb, bufs, hT_h, hT_r, col0)

            with ExitStack() as ph:
                xt = sb(ph, "xt", (128, 4, 1024), F32, n=1)
                nb_ss = sb(ph, "ss", (128, 4), F32, n=2)
                nb_junk = sb(ph, "junk", (128, 1024), BF16, n=1)
                nb_rstd = sb(ph, "rstd", (128, 4), F32, n=2)
                nb_xs = sb(ph, "xs", (128, 4, 1024), BF16, n=1)
                pT = ps(ph, "pT", (128, 1024), BF16, n=2)
                hT = sb(ph, "hT", (128, 8, 512), BF16, n=1)
                wsl = sb(ph, "wsl", (128, 8, 512), BF16, n=2)
                pz = ps(ph, "pz", (128, 512), F32, n=4)
                pn = ps(ph, "pn", (128, 512), F32, n=1)
                ptm = ps(ph, "ptm", (128, 512), F32, n=1)
                cq = sb(ph, "cq", (128, 5, 512), F32, n=1)
                sq = sb(ph, "sq", (128, 5, 512), BF16, n=1)
                cqn = sb(ph, "cqn", (128, 5, 512), BF16, n=1)
                rq = sb(ph, "rq", (128, 2, 512), F32, n=1)
                rA = sb(ph, "rA", (96, 2, 512), F32, n=1)
                rB = sb(ph, "rB", (128, 2, 512), F32, n=1)
                t1 = sb(ph, "t1", (128, 512), F32, n=2)
                t2 = sb(ph, "t2", (128, 512), F32, n=2)
                krs = sb(ph, "krs", (96, 512), BF16, n=2)
                qas = sb(ph, "qas", (96, 8, 512), BF16, n=1)
                kas = sb(ph, "kas", (64, 8, 512), BF16, n=1)
                qbs = sb(ph, "qbs", (128, 4, 512), BF16, n=2)
                kbs = sb(ph, "kbs", (128, 4, 512), BF16, n=2)
                gts = sb(ph, "gts", (128, 8, 512), BF16, n=2)
                vas = sb(ph, "vas", (128, 4, 8, 65), BF16, n=2)
                vbs = sb(ph, "vbs", (128, 4, 4, 129), BF16, n=2)
                wuq_h, wuq_r = sb(ph, "wuq", (128, 3, NUQ), BF16).next()
                wkk_h, wkk_r = sb(ph, "wkk", (128, 2, 512), BF16).next()
                wkv_h, wkv_r = sb(ph, "wkv", (128, 2, 512), BF16).next()
                ld(dma(wuq_h[:], wslab("wuq", l, 0, NUQ)), (WR["wuq"][l],), (wuq_r,))
                ld(dma(wkk_h[:], wslab("wukvk", l, 0, 512)), (WR["wukvk"][l],), (wkk_r,))
                ld(dma(wkv_h[:], wslab("wukvv", l, 0, 512)), (WR["wukvv"][l],), (wkv_r,))
                for i in range(2):
                    dve(OPF("memset", vas.h[i][:], 1.0), (), (vas.r[i],))
                    dve(OPF("memset", vbs.h[i][:], 1.0), (), (vbs.r[i],))
                nbufs = (nb_ss, nb_junk, nb_rstd, nb_xs, pT)
                GQh, GQr, GKVh, GKVr = GQ.h[0], GQ.r[0], GKV.h[0], GKV.r[0]

                def stage_a(tl, pre):
                    b, is_ctx, j, pos0, length, t0, T = tl
                    x_h, x_r = xt.next()
                    ld(dma(x_h[:, 0:T // 128, :], xsrc(b, is_ctx, t0, T).rearrange("(s p) d -> p s d", p=128)),
                       () if l == 0 else (RX2,), (x_r,))
                    pre["xs"] = norm_a(x_h, x_r, T // 128, nbufs)

                def stage_b(tl, pre):
                    b, is_ctx, j, pos0, length, t0, T = tl
                    pre["hT"] = hT.next()
                    norm_b(pre["xs"][0], pre["xs"][1], T // 128, T, j, A1, B1, nbufs, pre["hT"][0], pre["hT"][1])

                def tileA(tl, pre):
                    b, is_ctx, j, pos0, length, t0, T = tl
                    if True:
                        ns = T // 128
                        p0 = pos0 + t0
                        rA_h, rA_r = rA.next()
                        rB_h, rB_r = rB.next()
                        ld(dma(rA_h[64:96, :, 0:T], I["ropeA"][:, 64:96, p0:p0 + T].rearrange("c p t -> p c t")),
                           (), (rA_r,))
                        ld(dma(rB_h[:, :, 0:T], I["ropeB"][:, :, p0:p0 + T].rearrange("c p t -> p c t")),
                           (), (rB_r,))
                        hT_h, hT_r = pre["hT"]
                        cq_h, cq_r = cq.next(); sq_h, sq_r = sq.next(); cqn_h, cqn_r = cqn.next()
                        rq_h, rq_r = rq.next()
                        krs_h, krs_r = krs.next(); qas_h, qas_r = qas.next(); kas_h, kas_r = kas.next()
                        qbs_h, qbs_r = qbs.next(); kbs_h, kbs_r = kbs.next()
                        vas_h, vas_r = vas.next(); vbs_h, vbs_r = vbs.next()

                        def mm8(p_h, p_r, w_h, w_r, off, M):
                            for kc in range(8):
                                pe(OPF("matmul", p_h[0:M, 0:T], lhsT=w_h[:, kc, off:off + M],
                                                             rhs=hT_h[:, kc, 0:T], start=(kc == 0), stop=(kc == 7)),
                                   (w_r, hT_r), (p_r,))

                        def rope_evac(pm_h, pm_r, pp_h, pp_r, r_h, r_r, lo, hi, out_ap, out_r):
                            a_h, a_r = t1.next()
                            b_h, b_r = t2.next()
                            dve(OPF("tensor_tensor", out=a_h[lo:hi, 0:T], in0=pm_h[lo:hi, 0:T],
                                                          in1=r_h[lo:hi, 0, 0:T], op=ALU.mult), (pm_r, r_r), (a_r,))
                            dve(OPF("tensor_tensor", out=b_h[lo:hi, 0:T], in0=pp_h[lo:hi, 0:T],
                                                          in1=r_h[lo:hi, 1, 0:T], op=ALU.mult), (pp_r, r_r), (b_r,))
                            dve(OPF("tensor_tensor", out=out_ap, in0=a_h[lo:hi, 0:T], in1=b_h[lo:hi, 0:T],
                                                          op=ALU.add), (a_r, b_r), (out_r,))

                        for sl in range(10):
                            if sl == 2:
                                for (k0, k1, rr, nfeat) in ((0, 3, 0, Q_LORA), (3, 5, 1, KV_LORA)):
                                    p_h, p_r = pn.next()
                                    for kc in range(k0, k1):
                                        pe(OPF("matmul", p_h[:, 0:T], lhsT=ones_h[:], rhs=sq_h[:, kc, 0:T], start=(kc == k0),
                                            stop=(kc == k1 - 1)), (ones_r, sq_r), (p_r,))
                                    act(OPF("activation", out=rq_h[:, rr, 0:T], in_=p_h[:, 0:T], func=AF.Sqrt, scale=1.0 / nfeat,
                                            bias=EPS), (p_r,), (rq_r,))
                                    dve(OPF("reciprocal", out=rq_h[:, rr, 0:T], in_=rq_h[:, rr, 0:T]), (rq_r,), (rq_r,))
                                    for kc in range(k0, k1):
                                        g_h, g_r, gi = (GQh, GQr, kc) if rr == 0 else (GKVh, GKVr, kc - 3)
                                        dve(OPF("scalar_tensor_tensor", out=cqn_h[:, kc, 0:T], in0=cq_h[:, kc, 0:T], scalar=g_h[:, gi:gi + 1],
                                            in1=rq_h[:, rr, 0:T], op0=ALU.mult, op1=ALU.mult),
                                            (cq_r, rq_r, g_r), (cqn_r,))
                                yield
                            w_h, w_r = wsl.next()
                            ld(dma(w_h[:], wslab("win", l, sl * 512, sl * 512 + 512)), (WR["win"][l],), (w_r,))
                            if sl <= 1:
                                for cc in range(4 if sl == 0 else 1):
                                    ci = sl * 4 + cc
                                    p_h, p_r = pz.next()
                                    mm8(p_h, p_r, w_h, w_r, cc * 128, 128)
                                    act(OPF("activation", out=cq_h[:, ci, 0:T], in_=p_h[:, 0:T], func=AF.Copy), (p_r,), (cq_r,))
                                    act(OPF("activation", out=sq_h[:, ci, 0:T], in_=p_h[:, 0:T], func=AF.Square), (p_r,), (sq_r,))
                                if sl == 1:
                                    pm_h, pm_r = pz.next()
                                    mm8(pm_h, pm_r, w_h, w_r, 128, 96)
                                    pp_h, pp_r = pz.next()
                                    mm8(pp_h, pp_r, w_h, w_r, 256, 96)
                                    rope_evac(pm_h, pm_r, pp_h, pp_r, rA_h, rA_r, 64, 96, krs_h[64:96, 0:T], krs_r)
                                    st(dma(KR[b, :, p0:p0 + T], krs_h[64:96, 0:T]), (krs_r,), (RQKV,))
                            elif sl <= 5:
                                kind = (sl - 2) // 2
                                for hh in range(2):
                                    h = ((sl - 2) % 2) * 2 + hh
                                    pm_h, pm_r = pz.next()
                                    mm8(pm_h, pm_r, w_h, w_r, hh * 256, 128)
                                    pp_h, pp_r = pz.next()
                                    mm8(pp_h, pp_r, w_h, w_r, hh * 256 + 128, 128)
                                    dst_h, dst_r = (qbs_h, qbs_r) if kind == 0 else (kbs_h, kbs_r)
                                    rope_evac(pm_h, pm_r, pp_h, pp_r, rB_h, rB_r, 0, 128, dst_h[:, h, 0:T], dst_r)
                            else:
                                for cc in range(4):
                                    g = (sl - 6) * 4 + cc
                                    gk, gc = g // 8, g % 8
                                    if gc == 0:
                                        gts_h, gts_r = gts.next()
                                    p_h, p_r = pz.next()
                                    mm8(p_h, p_r, w_h, w_r, cc * 128, 128)
                                    act(OPF("activation", out=gts_h[:, gc, 0:T], in_=p_h[:, 0:T], func=AF.Sigmoid),
                                        (p_r,), (gts_r,))
                                    if gc == 7:
                                        st(dma(GT[b, gk, :, :, p0:p0 + T].rearrange("c p t -> p c t"),
                                               gts_h[:, :, 0:T]), (gts_r,), (RQKV,))
                        st(dma(QB[b, :, :, p0:p0 + T].rearrange("h p t -> p h t"), qbs_h[:, :, 0:T]), (qbs_r,), (RQKV,))
                        st(dma(KB[b, :, :, p0:p0 + T].rearrange("h p t -> p h t"), kbs_h[:, :, 0:T]), (kbs_r,), (RQKV,))
                        w_h, w_r = wsl.next()
                        ld(dma(w_h[:], wslab("win", l, 5120, 5632)), (WR["win"][l],), (w_r,))
                        for s in range(ns):
                            p_h, p_r = ptm.next()
                            for kc in range(8):
                                pe(OPF("matmul", p_h[:, :], lhsT=hT_h[:, kc, s * 128:(s + 1) * 128], rhs=w_h[:, kc, :],
                                    start=(kc == 0), stop=(kc == 7)), (w_r, hT_r), (p_r,))
                            dve(OPF("tensor_copy", out=vbs_h[:, s, :, 0:128], in_=p_h[:, :].rearrange("p (h d) -> p h d", h=4)),
                                (p_r,), (vbs_r,))
                        st(dma(VB[b, p0:p0 + T, :].rearrange("(s p) f -> p s f", p=128),
                               vbs_h[:, 0:ns].rearrange("p s h d -> p s (h d)")), (vbs_r,), (RQKV,))
                        yield
                        for h in range(H_A):
                            pm_h, pm_r = pz.next()
                            pp_h, pp_r = pz.next()
                            for (o_h, o_r, off) in ((pm_h, pm_r, h * 192), (pp_h, pp_r, h * 192 + 96)):
                                for kc in range(3):
                                    pe(OPF("matmul", o_h[0:96, 0:T], lhsT=wuq_h[:, kc, off:off + 96], rhs=cqn_h[:, kc, 0:T],
                                        start=(kc == 0), stop=(kc == 2)), (wuq_r, cqn_r), (o_r,))
                            act(OPF("activation", out=qas_h[0:64, h, 0:T], in_=pm_h[0:64, 0:T], func=AF.Copy), (pm_r,), (qas_r,))
                            rope_evac(pm_h, pm_r, pp_h, pp_r, rA_h, rA_r, 64, 96, qas_h[64:96, h, 0:T], qas_r)
                            pk_h, pk_r = pz.next()
                            for kc in range(2):
                                pe(OPF("matmul", pk_h[0:64, 0:T], lhsT=wkk_h[:, kc, h * 64:(h + 1) * 64], rhs=cqn_h[:, 3 + kc, 0:T],
                                    start=(kc == 0), stop=(kc == 1)), (wkk_r, cqn_r), (pk_r,))
                            act(OPF("activation", out=kas_h[:, h, 0:T], in_=pk_h[0:64, 0:T], func=AF.Copy), (pk_r,), (kas_r,))
                        st(dma(QA[b, :, :, p0:p0 + T].rearrange("h p t -> p h t"), qas_h[:, :, 0:T]), (qas_r,), (RQKV,))
                        st(dma(KA[b, :, :, p0:p0 + T].rearrange("h p t -> p h t"), kas_h[:, :, 0:T]), (kas_r,), (RQKV,))
                        for s in range(ns):
                            p_h, p_r = ptm.next()
                            for kc in range(2):
                                pe(OPF("matmul", p_h[:, :], lhsT=cqn_h[:, 3 + kc, s * 128:(s + 1) * 128], rhs=wkv_h[:, kc, :],
                                    start=(kc == 0), stop=(kc == 1)), (wkv_r, cqn_r), (p_r,))
                            dve(OPF("tensor_copy", out=vas_h[:, s, :, 0:64], in_=p_h[:, :].rearrange("p (h d) -> p h d", h=8)),
                                (p_r,), (vas_r,))
                        st(dma(VA[b, p0:p0 + T, :].rearrange("(s p) f -> p s f", p=128),
                               vas_h[:, 0:ns].rearrange("p s h d -> p s (h d)")), (vas_r,), (RQKV,))

                tlist = [(b, is_ctx, j, pos0, length, t0, T) for (b, is_ctx, j, pos0, length) in seqs
                         for (t0, T) in tiles_of(length)]
                pres = [dict() for _ in tlist]
                stage_a(tlist[0], pres[0])
                stage_b(tlist[0], pres[0])
                for ti, tl in enumerate(tlist):
                    g = tileA(tl, pres[ti])
                    next(g)
                    if ti + 1 < len(tlist):
                        stage_a(tlist[ti + 1], pres[ti + 1])
                    next(g)
                    if ti + 1 < len(tlist):
                        stage_b(tlist[ti + 1], pres[ti + 1])
                    for _ in g:
                        pass
                SC.barrier()
                SC.flush()
            if debug_stop == "A":
                break

            scaleA = float((NOPE_A + ROPE_A) ** -0.5)
            scaleB = float(DH_B ** -0.5)
            with ExitStack() as ph:
                vall = sb(ph, "vall", (128, NKT, 520), BF16, n=1)
                kT = sb(ph, "kT", (128, P), BF16, n=2)
                qT = sb(ph, "qT", (128, P), BF16, n=2)
                pTt = sb(ph, "pTt", (128, 2, 512), BF16, n=3)
                osb = sb(ph, "osb", (128, NKT, 1024), BF16, n=1)
                rl = sb(ph, "rl", (128, 8), F32, n=2)
                tt = sb(ph, "tt", (128, 128), F32, n=2)
                oo = sb(ph, "oo", (128, 128), F32, n=2)
                jk2 = sb(ph, "jk2", (128, 128), F32, n=1)
                ss2 = sb(ph, "ss2", (128, 2), F32, n=2)
                scp = ps(ph, "scp", (128, 2, 512), F32, n=2)
                accb = ps(ph, "accb", (128, 512), F32, n=4)
                LAMh, LAMr = LAM.h[0], LAM.r[0]
                GSBh, GSBr = GSB.h[0], GSB.r[0]
                for b in range(NB):
                    qblocks = ([] if last else [(0, CTX, CTX // 128)]) + [(CTX + i * 512, 512, NKT) for i in range(S // 512)]
                    o_h, o_r = osb.next()
                    v_h, v_r = vall.next()
                    ld(dma(v_h[:, :, :], VA[b].rearrange("(t p) f -> p t f", p=128)), (RQKV,), (v_r,))
                    for h in range(H_A):
                        k_h, k_r = kT.next()
                        q_h, q_r = qT.next()
                        ld(dma(k_h[0:64, :], KA[b, h]), (RQKV,), (k_r,))
                        ld(dma(k_h[64:96, :], KR[b]), (RQKV,), (k_r,))
                        ld(dma(q_h[0:96, :], QA[b, h]), (RQKV,), (q_r,))
                        units = []
                        for qi, (q0, nq, nkt) in enumerate(qblocks):
                            for kp in range(0, nkt, 2):
                                units.append((qi, q0, nq, nkt, kp))

                        def qk_mla(u):
                            qi, q0, nq, nkt, kp = u
                            s_h, s_r = scp.next()
                            for uu in range(2):
                                kt = kp + uu
                                pe(OPF("matmul", s_h[:, uu, 0:nq], lhsT=k_h[0:96, kt * 128:(kt + 1) * 128],
                                       rhs=q_h[0:96, q0:q0 + nq], start=True, stop=True), (k_r, q_r), (s_r,))
                            p_h, p_r = pTt.next()
                            act(OPF("activation", out=p_h[:, :, 0:nq], in_=s_h[:, :, 0:nq], func=AF.Exp, scale=scaleA),
                                (s_r,), (p_r,))
                            return p_h, p_r

                        def pv_mla(u, pp):
                            qi, q0, nq, nkt, kp = u
                            p_h, p_r = pp
                            nj = nq // 128
                            a_h, a_r = accb.h[qi % 4], accb.r[qi % 4]
                            for uu in range(2):
                                kt = kp + uu
                                for jj in range(nj):
                                    pe(OPF("matmul", a_h[:, jj * 65:(jj + 1) * 65], lhsT=p_h[:, uu, jj * 128:(jj + 1) * 128],
                                           rhs=v_h[:, kt, h * 65:(h + 1) * 65], start=(kt == 0 and jj == 0), stop=(kt == nkt - 1),
                                           skip_group_check=True), (p_r, v_r), (a_r,))
                            if kp + 2 >= nkt:
                                r_h, r_r = rl.next()
                                for jj in range(nj):
                                    qt = q0 // 128 + jj
                                    dve(OPF("reciprocal", out=r_h[:, jj:jj + 1], in_=a_h[:, jj * 65 + 64:jj * 65 + 65]), (a_r,), (r_r,))
                                    dve(OPF("tensor_scalar", out=o_h[:, qt, h * 64:(h + 1) * 64], in0=a_h[:, jj * 65:jj * 65 + 64],
                                            scalar1=r_h[:, jj:jj + 1], scalar2=None, op0=ALU.mult), (a_r, r_r), (o_r,))

                        prev = None
                        for u in units:
                            cur = qk_mla(u)
                            if prev is not None:
                                pv_mla(*prev)
                            prev = (u, cur)
                        pv_mla(*prev)
                    v_h, v_r = vall.next()
                    ld(dma(v_h[:, :, 0:516], VB[b].rearrange("(t p) f -> p t f", p=128)), (RQKV,), (v_r,))
                    for h in range(H_B):
                        k_h, k_r = kT.next()
                        q_h, q_r = qT.next()
                        ld(dma(k_h[:, :], KB[b, h]), (RQKV,), (k_r,))
                        ld(dma(q_h[:, :], QB[b, h]), (RQKV,), (q_r,))
                        units = []
                        for qi, (q0, nq, nkt) in enumerate(qblocks):
                            for kt in range(nkt):
                                units.append((qi, q0, nq, nkt, kt))
                        banks = [(accb.h[k_], accb.r[k_]) for k_ in range(4)]

                        def qk_diff(u):
                            qi, q0, nq, nkt, kt = u
                            s_h, s_r = scp.next()
                            for m in range(2):
                                pe(OPF("matmul", s_h[:, m, 0:nq], lhsT=k_h[m * 64:(m + 1) * 64, kt * 128:(kt + 1) * 128],
                                       rhs=q_h[m * 64:(m + 1) * 64, q0:q0 + nq], start=True, stop=True), (k_r, q_r), (s_r,))
                            p_h, p_r = pTt.next()
                            act(OPF("activation", out=p_h[:, :, 0:nq], in_=s_h[:, :, 0:nq], func=AF.Exp, scale=scaleB),
                                (s_r,), (p_r,))
                            return p_h, p_r

                        def pv_diff(u, pp):
                            qi, q0, nq, nkt, kt = u
                            p_h, p_r = pp
                            nj = nq // 128
                            for m in range(2):
                                for jj in range(nj):
                                    bk_h, bk_r = banks[m * 2 + jj // 2]
                                    off = (jj % 2) * 129
                                    pe(OPF("matmul", bk_h[:, off:off + 129], lhsT=p_h[:, m, jj * 128:(jj + 1) * 128],
                                           rhs=v_h[:, kt, h * 129:(h + 1) * 129], start=(kt == 0 and jj % 2 == 0),
                                           stop=(kt == nkt - 1), skip_group_check=True), (p_r, v_r), (bk_r,))
                            if kt != nkt - 1:
                                return
                            for jj in range(nj):
                                qt = q0 // 128 + jj
                                b1_h, b1_r = banks[jj // 2]
                                b2_h, b2_r = banks[2 + jj // 2]
                                off = (jj % 2) * 129
                                r_h, r_r = rl.next()
                                t_h, t_r = tt.next()
                                oo_h, oo_r = oo.next()
                                s2_h, s2_r = ss2.next()
                                jk_h, jk_r = jk2.next()
                                dve(OPF("reciprocal", out=r_h[:, 0:1], in_=b1_h[:, off + 128:off + 129]), (b1_r,), (r_r,))
                                dve(OPF("reciprocal", out=r_h[:, 1:2], in_=b2_h[:, off + 128:off + 129]), (b2_r,), (r_r,))
                                dve(OPF("tensor_tensor", out=r_h[:, 1:2], in0=r_h[:, 1:2], in1=LAMh[:, 0:1], op=ALU.mult),
                                    (r_r, LAMr), (r_r,))
                                dve(OPF("tensor_scalar", out=t_h[:], in0=b2_h[:, off:off + 128], scalar1=r_h[:, 1:2], scalar2=None,
                                        op0=ALU.mult), (b2_r, r_r), (t_r,))
                                dve(OPF("scalar_tensor_tensor", out=oo_h[:], in0=b1_h[:, off:off + 128], scalar=r_h[:, 0:1],
                                        in1=t_h[:], op0=ALU.mult, op1=ALU.subtract), (b1_r, r_r, t_r), (oo_r,))
                                dve(OPF("memset", s2_h[:], 0.0), (), (s2_r,))
                                act(OPF("activation", out=jk_h[:], in_=oo_h[:], func=AF.Square, accum_out=s2_h[:, 0:1]),
                                    (oo_r, s2_r), (jk_r, s2_r))
                                act(OPF("activation", out=s2_h[:, 1:2], in_=s2_h[:, 0:1], func=AF.Ln, scale=1.0 / V_B, bias=EPS),
                                    (s2_r,), (s2_r,))
                                act(OPF("activation", out=s2_h[:, 1:2], in_=s2_h[:, 1:2], func=AF.Exp, scale=-0.5),
                                    (s2_r,), (s2_r,))
                                dve(OPF("scalar_tensor_tensor", out=o_h[:, qt, 512 + h * 128:512 + (h + 1) * 128], in0=oo_h[:],
                                        scalar=s2_h[:, 1:2], in1=GSBh[:], op0=ALU.mult, op1=ALU.mult),
                                    (oo_r, s2_r, GSBr), (o_r,))

                        prev = None
                        for u in units:
                            cur = qk_diff(u)
                            if prev is not None:
                                pv_diff(*prev)
                            prev = (u, cur)
                        pv_diff(*prev)
                    qt0 = 0 if not last else CTX // 128
                    for (ta, tb) in ((qt0, NKT // 2), (NKT // 2, NKT)):
                        st(dma(OO[b, ta * 128:tb * 128, :].rearrange("(t p) f -> p t f", p=128), o_h[:, ta:tb, :]),
                           (o_r,), (ROO,))
                SC.barrier()
                SC.flush()
            if debug_stop == "B":
                break

            with ExitStack() as ph:
                ot = sb(ph, "ot", (128, 4, 1024), BF16, n=1)
                oT = sb(ph, "oT", (128, 8, 512), BF16, n=1)
                sg = sb(ph, "sg", (128, 2, 8, 512), BF16, n=1)
                mT = sb(ph, "mT", (128, 8, 512), BF16, n=1)
                t1 = sb(ph, "c1t1", (128, 512), F32, n=2)
                t2 = sb(ph, "c1t2", (128, 512), F32, n=2)
                xt = sb(ph, "c1xt", (128, 4, 1024), F32, n=1)
                xn = sb(ph, "c1xn", (128, 4, 1024), F32, n=1)
                ttm = sb(ph, "c1ttm", (128, 512), F32, n=2)
                nb_ss = sb(ph, "c1ss", (128, 4), F32, n=2)
                nb_junk = sb(ph, "c1junk", (128, 1024), BF16, n=1)
                nb_rstd = sb(ph, "c1rstd", (128, 4), F32, n=2)
                nb_xs = sb(ph, "c1xs", (128, 4, 1024), BF16, n=1)
                h2 = sb(ph, "c1h2", (128, 8, 512), BF16, n=2)
                zt_h, zt_r = sb(ph, "c1zt", (128, 8, 1), BF16).next()
                pT = ps(ph, "c1pT", (128, 1024), BF16, n=2)
                py = ps(ph, "c1py", (128, 512), F32, n=4)
                po = ps(ph, "c1po", (128, 512), F32, n=2)
                wa_h, wa_r = sb(ph, "c1wa", (128, 4, 1024), BF16).next()
                wb_h, wb_r = sb(ph, "c1wb", (128, 4, 1024), BF16).next()
                wo_h, wo_r = sb(ph, "c1wo", (128, 8, 1024), BF16).next()
                ld(dma(wa_h[:], wslab("wbra", l, 0, 1024)), (WR["wbra"][l],), (wa_r,))
                ld(dma(wb_h[:], wslab("wbrb", l, 0, 1024)), (WR["wbrb"][l],), (wb_r,))
                ld(dma(wo_h[:], wslab("wout", l, 0, 1024)), (WR["wout"][l],), (wo_r,))
                dve(OPF("memset", zt_h[:], 0.0), (), (zt_r,))
                nbufs = (nb_ss, nb_junk, nb_rstd, nb_xs, pT)
                GTBh, GTBr = GTB.h[0], GTB.r[0]
                def tileC1(tl):
                    b, is_ctx, j, pos0, length, t0, T = tl
                    si = 0 if is_ctx else 1
                    if t0 == 0:
                        for cpad in (0, length + 1):
                            st(OPF("dma_start", out=H2[b, si, :, :, cpad:cpad + 1].rearrange("c p t -> p c t"), in_=zt_h[:],
                                   allow_slow_non_contiguous=True), (zt_r,), (RH2,))
                    if True:
                        ns = T // 128
                        p0 = pos0 + t0
                        ot_h, ot_r = ot.next(); sg_h, sg_r = sg.next(); x_h, x_r = xt.next()
                        ld(dma(ot_h[:, 0:ns, :], OO[b, p0:p0 + T, :].rearrange("(s p) f -> p s f", p=128)), (ROO,), (ot_r,))
                        for g2 in range(2):
                            ld(dma(sg_h[:, g2, :, 0:T], GT[b, g2, :, :, p0:p0 + T].rearrange("c p t -> p c t")), (RQKV,), (sg_r,))
                        ld(dma(x_h[:, 0:ns, :], xsrc(b, is_ctx, t0, T).rearrange("(s p) d -> p s d", p=128)),
                           () if l == 0 else (RX2,), (x_r,))
                        oT_h, oT_r = oT.next()
                        for c2 in range(4):
                            p_h, p_r = pT.next()
                            for cc in range(2):
                                c = c2 * 2 + cc
                                for s in range(ns):
                                    pe(OPF("transpose", out=p_h[:, cc * 512 + s * 128: cc * 512 + (s + 1) * 128],
                                           in_=ot_h[:, s, c * 128:(c + 1) * 128], identity=ident_h[:]), (ot_r, ident_r), (p_r,))
                            for cc in range(2):
                                c = c2 * 2 + cc
                                if cc == 0:
                                    act(OPF("activation", out=oT_h[:, c, 0:T], in_=p_h[:, cc * 512:cc * 512 + T], func=AF.Copy),
                                        (p_r,), (oT_r,))
                                else:
                                    dve(OPF("tensor_copy", out=oT_h[:, c, 0:T], in_=p_h[:, cc * 512:cc * 512 + T]), (p_r,), (oT_r,))
                        mT_h, mT_r = mT.next()
                        for c in range(8):
                            pa_h, pa_r = py.next()
                            pb_h, pb_r = py.next()
                            for kc in range(4):
                                pe(OPF("matmul", pa_h[:, 0:T], lhsT=wa_h[:, kc, c * 128:(c + 1) * 128], rhs=oT_h[:, kc, 0:T],
                                       start=(kc == 0), stop=(kc == 3)), (wa_r, oT_r), (pa_r,))
                            for kc in range(4):
                                pe(OPF("matmul", pb_h[:, 0:T], lhsT=wb_h[:, kc, c * 128:(c + 1) * 128], rhs=oT_h[:, 4 + kc, 0:T],
                                       start=(kc == 0), stop=(kc == 3)), (wb_r, oT_r), (pb_r,))
                            a_h, a_r = t1.next()
                            b_h, b_r = t2.next()
                            dve(OPF("tensor_tensor", out=a_h[:, 0:T], in0=pa_h[:, 0:T], in1=sg_h[:, 0, c, 0:T], op=ALU.mult),
                                (pa_r, sg_r), (a_r,))
                            dve(OPF("tensor_tensor", out=b_h[:, 0:T], in0=pb_h[:, 0:T], in1=sg_h[:, 1, c, 0:T], op=ALU.mult),
                                (pb_r, sg_r), (b_r,))
                            dve(OPF("tensor_tensor", out=mT_h[:, c, 0:T], in0=a_h[:, 0:T], in1=b_h[:, 0:T], op=ALU.add),
                                (a_r, b_r), (mT_r,))
                        yield
                        xn_h, xn_r = xn.next()
                        for s in range(ns):
                            for half in range(2):
                                p_h, p_r = po.next()
                                for kc in range(8):
                                    pe(OPF("matmul", p_h[:, :], lhsT=mT_h[:, kc, s * 128:(s + 1) * 128],
                                           rhs=wo_h[:, kc, half * 512:(half + 1) * 512], start=(kc == 0), stop=(kc == 7)),
                                       (wo_r, mT_r), (p_r,))
                                m_h, m_r = ttm.next()
                                dve(OPF("tensor_tensor", out=m_h[:], in0=p_h[:, :], in1=GTBh[:, j, 0, half * 512:(half + 1) * 512],
                                        op=ALU.mult), (p_r, GTBr), (m_r,))
                                dve(OPF("tensor_tensor", out=xn_h[:, s, half * 512:(half + 1) * 512], in0=m_h[:],
                                        in1=x_h[:, s, half * 512:(half + 1) * 512], op=ALU.add), (m_r, x_r), (xn_r,))
                        st(dma(X1[b, p0:p0 + T, :].rearrange("(s p) d -> p s d", p=128), xn_h[:, 0:ns, :]), (xn_r,), (RX1,))
                        xs_h, xs_r = norm_a(xn_h, xn_r, ns, nbufs)
                        yield
                        h2_h, h2_r = h2.next()
                        norm_b(xs_h, xs_r, ns, T, j, A2, B2, nbufs, h2_h, h2_r)
                        st(dma(H2[b, si, :, :, 1 + t0:1 + t0 + T].rearrange("c p t -> p c t"), h2_h[:, :, 0:T]), (h2_r,), (RH2,))

                tlist = [(b, is_ctx, j, pos0, length, t0, T) for (b, is_ctx, j, pos0, length) in seqs
                         if not (is_ctx and last) for (t0, T) in tiles_of(length)]
                gens = [tileC1(tl) for tl in tlist]
                next(gens[0])
                for ti in range(len(tlist)):
                    next(gens[ti])
                    if ti + 1 < len(tlist):
                        next(gens[ti + 1])
                    for _ in gens[ti]:
                        pass
                SC.barrier()
                SC.flush()
            if debug_stop == "C1":
                break

            with ExitStack() as ph:
                h2 = sb(ph, "c2h2", (128, 8, 514), BF16, n=2)
                wsl = sb(ph, "c2wsl", (128, 8, 512), BF16, n=3)
                gT = sb(ph, "c2gT", (128, 22, 512), BF16, n=1)
                wd = sb(ph, "c2wd", (128, 22, 512), BF16, n=2)
                ua = sb(ph, "c2ua", (128, 512), F32, n=2)
                ub = sb(ph, "c2ub", (128, 512), F32, n=2)
                sa = sb(ph, "c2sa", (128, 512), F32, n=2)
                xm = sb(ph, "c2xm", (128, 4, 1024), F32, n=1)
                xo = sb(ph, "c2xo", (128, 4, 1024), F32, n=1)
                xf = sb(ph, "c2xf", (128, 4, 1024), F32, n=1)
                ttm = sb(ph, "c2ttm", (128, 512), F32, n=2)
                fss = sb(ph, "c2ss", (128, 4), F32, n=2)
                frs = sb(ph, "c2rs", (128, 4), F32, n=2)
                fjk = sb(ph, "c2jk", (128, 1024), BF16, n=1)
                pu = ps(ph, "c2pu", (128, 1024), F32, n=3)
                pd = ps(ph, "c2pd", (128, 512), F32, n=2)
                CWh, CWr, CBh, CBr = CW.h[0], CW.r[0], CB.h[0], CB.r[0]
                GTBh, GTBr = GTB.h[0], GTB.r[0]
                for (b, is_ctx, j, pos0, length) in seqs:
                    if is_ctx and last:
                        continue
                    si = 0 if is_ctx else 1
                    for (t0, T) in tiles_of(length):
                        ns = T // 128
                        p0 = pos0 + t0
                        h_h, h_r = h2.next()
                        x_h, x_r = xm.next()
                        ld(dma(h_h[:, :, 0:T + 2], H2[b, si, :, :, t0:t0 + T + 2].rearrange("c p t -> p c t")), (RH2,), (h_r,))
                        ld(dma(x_h[:, 0:ns, :], X1[b, p0:p0 + T, :].rearrange("(s p) d -> p s d", p=128)), (RX1,), (x_r,))
                        g_h, g_r = gT.next()
                        for i in range(22):
                            if i % 2 == 0:
                                w_h, w_r = wsl.next()
                                ld(dma(w_h[:], wslab("wup", l, (i // 2) * 512, (i // 2) * 512 + 512)), (WR["wup"][l],), (w_r,))
                            us = []
                            for which in range(2):
                                ch = 2 * i + which
                                off = ((i % 2) * 2 + which) * 128
                                p_h, p_r = pu.next()
                                for kc in range(8):
                                    pe(OPF("matmul", p_h[:, 0:T], lhsT=w_h[:, kc, off:off + 128], rhs=h_h[:, kc, 0:T],
                                           start=(kc == 0), stop=(kc == 7)), (w_r, h_r), (p_r,))
                                for kc in range(8):
                                    pe(OPF("matmul", p_h[:, T:T + 2], lhsT=w_h[:, kc, off:off + 128], rhs=h_h[:, kc, T:T + 2],
                                           start=(kc == 0), stop=(kc == 7)), (w_r, h_r), (p_r,))
                                u_h, u_r = (ua if which == 0 else ub).next()
                                act(OPF("activation", out=u_h[:, 0:T], in_=p_h[:, 1:T + 1], func=AF.Identity,
                                        scale=CWh[:, ch, 1:2], bias=CBh[:, ch:ch + 1]), (p_r, CWr, CBr), (u_r,))
                                dve(OPF("scalar_tensor_tensor", out=u_h[:, 0:T], in0=p_h[:, 0:T], scalar=CWh[:, ch, 0:1],
                                        in1=u_h[:, 0:T], op0=ALU.mult, op1=ALU.add), (p_r, CWr, u_r), (u_r,))
                                dve(OPF("scalar_tensor_tensor", out=u_h[:, 0:T], in0=p_h[:, 2:T + 2], scalar=CWh[:, ch, 2:3],
                                        in1=u_h[:, 0:T], op0=ALU.mult, op1=ALU.add), (p_r, CWr, u_r), (u_r,))
                                us.append((u_h, u_r))
                            s_h, s_r = sa.next()
                            act(OPF("activation", out=s_h[:, 0:T], in_=us[0][0][:, 0:T], func=AF.Silu), (us[0][1],), (s_r,))
                            pool(OPF("tensor_tensor", out=g_h[:, i, 0:T], in0=s_h[:, 0:T], in1=us[1][0][:, 0:T], op=ALU.mult),
                                 (s_r, us[1][1]), (g_r,))
                        xo_h, xo_r = xo.next()
                        for half in range(2):
                            d_h, d_r = wd.next()
                            ld(dma(d_h[:], wslab("wdown", l, half * 512, half * 512 + 512)), (WR["wdown"][l],), (d_r,))
                            for s in range(ns):
                                p_h, p_r = pd.next()
                                for kc in range(22):
                                    pe(OPF("matmul", p_h[:, :], lhsT=g_h[:, kc, s * 128:(s + 1) * 128], rhs=d_h[:, kc, :],
                                           start=(kc == 0), stop=(kc == 21)), (d_r, g_r), (p_r,))
                                m_h, m_r = ttm.next()
                                dve(OPF("tensor_tensor", out=m_h[:], in0=p_h[:, :], in1=GTBh[:, j, 1, half * 512:(half + 1) * 512],
                                        op=ALU.mult), (p_r, GTBr), (m_r,))
                                dve(OPF("tensor_tensor", out=xo_h[:, s, half * 512:(half + 1) * 512], in0=m_h[:],
                                        in1=x_h[:, s, half * 512:(half + 1) * 512], op=ALU.add), (m_r, x_r), (xo_r,))
                        if not last:
                            st(dma(X2[b, p0:p0 + T, :].rearrange("(s p) d -> p s d", p=128), xo_h[:, 0:ns, :]), (xo_r,), (RX2,))
                        else:
                            ss_h, ss_r = fss.next(); rs_h, rs_r = frs.next(); jk_h, jk_r = fjk.next(); xf_h, xf_r = xf.next()
                            dve(OPF("memset", ss_h[:], 0.0), (), (ss_r,))
                            for s in range(ns):
                                act(OPF("activation", out=jk_h[:], in_=xo_h[:, s, :], func=AF.Square, accum_out=ss_h[:, s:s + 1]),
                                    (xo_r, ss_r), (jk_r, ss_r))
                            act(OPF("activation", out=rs_h[:, 0:ns], in_=ss_h[:, 0:ns], func=AF.Sqrt, scale=1.0 / D, bias=EPS),
                                (ss_r,), (rs_r,))
                            dve(OPF("reciprocal", out=rs_h[:, 0:ns], in_=rs_h[:, 0:ns]), (rs_r,), (rs_r,))
                            for s in range(ns):
                                dve(OPF("scalar_tensor_tensor", out=xf_h[:, s, :], in0=xo_h[:, s, :], scalar=rs_h[:, s:s + 1],
                                        in1=gfin_h[:], op0=ALU.mult, op1=ALU.mult), (xo_r, rs_r, gfin_r), (xf_r,))
                            st(dma(out[b, t0:t0 + T, :].rearrange("(s p) d -> p s d", p=128), xf_h[:, 0:ns, :]), (xf_r,), ())
                SC.barrier()
                SC.flush()
            if debug_stop == "C2" + str(l):
                break
    return nc


_CACHE = {}


def kernel(**inputs):
    S = inputs["x"].shape[1]
    CTX = inputs["ctx"].shape[1]
    B = inputs["x"].shape[0]
    ncores = 8
    nb = B // ncores
    nc = build_program(S, CTX, nb)
    shared = prep_shared(inputs, S, CTX)
    in_maps = []
    for c in range(ncores):
        m = dict(shared)
        m.update(prep_core(inputs, c, nb))
        in_maps.append(m)
    res = run_bass_kernel_spmd(nc, in_maps, core_ids=list(range(ncores)))
    return np.concatenate([np.asarray(r["out"]) for r in res.results], axis=0).astype(np.float32)
```
